# Optimizing a Trainium2 kernel written in Bass

```python
import math
import jax
import jax.numpy as jnp
from jax import lax
import numpy as np


D_MODEL = 1024
BATCH = 16
SEQ = 2048
DEPTH = 2

N_A_LAYERS = DEPTH // 2
N_B_LAYERS = DEPTH - N_A_LAYERS

SSM_GROUP = 16
SSM_GROUPS = D_MODEL // SSM_GROUP
SSM_STATE = 64
DT_MIN = 0.001
DT_MAX = 0.1

HEAD_DIM = 64
HEADS_PER_GROUP = D_MODEL // HEAD_DIM
DILATED_GROUPS = ((128, 1), (512, 4), (2048, 16))
N_DIL = len(DILATED_GROUPS)
BAND = DILATED_GROUPS[0][0] // DILATED_GROUPS[0][1]
MAX_DIL = max(d for _, d in DILATED_GROUPS)
ATT_WIDTH = N_DIL * HEADS_PER_GROUP * HEAD_DIM
MERGED_WIDTH = HEADS_PER_GROUP * HEAD_DIM
NEG_BIG = -1e30

REL_BUCKETS = 32
REL_MAX_DIST = 2048

D_FF = 2816
CONV_WIDTH = 3

DN_ALPHA = (2.0 * DEPTH) ** 0.25
DN_BETA = (8.0 * DEPTH) ** -0.25
LN_EPS = 1e-5

kernel_name = 'yoco_s5_dilated_attn_deepnorm_trunk'


def layer_norm(x, gain, bias):
    xf = x.astype(jnp.float32)
    mu = jnp.mean(xf, axis=-1, keepdims=True)
    var = jnp.mean(jnp.square(xf - mu), axis=-1, keepdims=True)
    y = (xf - mu) * lax.rsqrt(var + LN_EPS) * gain.astype(jnp.float32) + bias.astype(jnp.float32)
    return y.astype(x.dtype)


def post_norm(x, f, gain, bias):
    return layer_norm(DN_ALPHA * x + f.astype(x.dtype), gain, bias)


def _complex_affine_combine(e1, e2):
    a1r, a1i, b1r, b1i = e1
    a2r, a2i, b2r, b2i = e2
    return (a2r * a1r - a2i * a1i,
            a2r * a1i + a2i * a1r,
            a2r * b1r - a2i * b1i + b2r,
            a2r * b1i + a2i * b1r + b2i)


def s5_mixer(x, lam_re, lam_im, log_dt, b_re, b_im, c_re, c_im, d_skip, w_glu, b_glu, w_out):
    f32 = jnp.float32
    bsz, seq, _ = x.shape
    u = x.astype(f32).reshape(bsz, seq, SSM_GROUPS, SSM_GROUP)
    lr, li = lam_re.astype(f32), lam_im.astype(f32)
    dt = jnp.exp(log_dt.astype(f32))[:, None]
    mag = jnp.exp(lr * dt)
    ab_r, ab_i = mag * jnp.cos(li * dt), mag * jnp.sin(li * dt)
    den = lr * lr + li * li
    nr = ab_r - 1.0
    co_r = (nr * lr + ab_i * li) / den
    co_i = (ab_i * lr - nr * li) / den
    br, bi = b_re.astype(f32), b_im.astype(f32)
    bb_r = co_r[..., None] * br - co_i[..., None] * bi
    bb_i = co_r[..., None] * bi + co_i[..., None] * br
    bu_r = jnp.einsum('blgh,gph->blgp', u, bb_r)
    bu_i = jnp.einsum('blgh,gph->blgp', u, bb_i)
    a_r = jnp.broadcast_to(ab_r, (1, seq, SSM_GROUPS, SSM_STATE))
    a_i = jnp.broadcast_to(ab_i, (1, seq, SSM_GROUPS, SSM_STATE))
    _, _, h_r, h_i = lax.associative_scan(_complex_affine_combine, (a_r, a_i, bu_r, bu_i), axis=1)
    y = (jnp.einsum('blgp,ghp->blgh', h_r, c_re.astype(f32))
         - jnp.einsum('blgp,ghp->blgh', h_i, c_im.astype(f32))
         + d_skip.astype(f32) * u)
    y = jax.nn.gelu(y.reshape(bsz, seq, D_MODEL))
    g = y * jax.nn.sigmoid(y @ w_glu.astype(f32) + b_glu.astype(f32))
    return (g @ w_out.astype(f32)).astype(x.dtype)


def conv_glu_ffn(x, w_up, conv_w, conv_b, w_down):
    seq = x.shape[1]
    hcat = x @ w_up
    hp = jnp.pad(hcat, ((0, 0), (CONV_WIDTH - 1, 0), (0, 0)))
    hcat = conv_b + sum(conv_w[k] * hp[:, CONV_WIDTH - 1 - k:CONV_WIDTH - 1 - k + seq]
                        for k in range(CONV_WIDTH))
    val, gate = jnp.split(hcat, 2, axis=-1)
    return (jax.nn.silu(gate) * val) @ w_down


def _padded_len(seq):
    span = BAND * MAX_DIL
    return -(-seq // span) * span


def _to_residue_blocks(t, dil):
    bsz, lp, h, e = t.shape
    m = lp // dil
    t = t.reshape(bsz, m, dil, h, e).transpose(0, 2, 1, 3, 4)
    return t.reshape(bsz, dil, m // BAND, BAND, h, e)


def _from_residue_blocks(t, dil):
    bsz = t.shape[0]
    rest = t.shape[4:]
    t = t.reshape((bsz, dil, -1) + rest)
    t = jnp.moveaxis(t, 1, 2)
    return t.reshape((bsz, -1) + rest)


def _with_previous_block(t):
    prev = jnp.concatenate([jnp.zeros_like(t[:, :, :1]), t[:, :, :-1]], axis=2)
    return jnp.concatenate([prev, t], axis=3)


def _t5_bucket(dist):
    exact = REL_BUCKETS // 2
    d = np.maximum(dist, 1).astype(np.float32)
    large = exact + (np.log(d / exact) / math.log(REL_MAX_DIST / exact)
                     * (REL_BUCKETS - exact)).astype(np.int64)
    large = np.minimum(large, REL_BUCKETS - 1)
    return np.where(dist < exact, dist, large).astype(np.int32)


def _group_bias_mask(rel_bias, g, dil, n_blocks):
    steps = np.arange(BAND)[:, None] + BAND - np.arange(2 * BAND)[None, :]
    bucket = _t5_bucket(np.maximum(steps, 0) * dil)
    cols = rel_bias[:, g * HEADS_PER_GROUP:(g + 1) * HEADS_PER_GROUP]
    bias = jnp.transpose(cols[bucket], (2, 0, 1)).astype(jnp.float32)
    in_band = (steps >= 0) & (steps <= BAND)
    has_prev = (np.arange(n_blocks)[:, None, None] > 0) | (np.arange(2 * BAND)[None, None, :] >= BAND)
    valid = jnp.asarray(in_band[None] & has_prev)
    return bias, valid


def shared_kv(h, w_kv):
    bsz, seq, _ = h.shape
    lp = _padded_len(seq)
    kv = (h @ w_kv).astype(jnp.float32)
    kv = jnp.pad(kv, ((0, 0), (0, lp - seq), (0, 0)))
    kv = kv.reshape(bsz, lp, 2, N_DIL, HEADS_PER_GROUP, HEAD_DIM)
    blocks = []
    for g, (_, dil) in enumerate(DILATED_GROUPS):
        blocks.append(_with_previous_block(_to_residue_blocks(kv[:, :, 0, g], dil)))
        blocks.append(_with_previous_block(_to_residue_blocks(kv[:, :, 1, g], dil)))
    return blocks


def dilated_attention(h, w_q, w_out, rel_bias, kv_blocks):
    bsz, seq, _ = h.shape
    lp = _padded_len(seq)
    q = (h @ w_q).astype(jnp.float32) * (HEAD_DIM ** -0.5)
    q = jnp.pad(q, ((0, 0), (0, lp - seq), (0, 0))).reshape(bsz, lp, N_DIL, HEADS_PER_GROUP, HEAD_DIM)
    outs, lses = [], []
    for g, (_, dil) in enumerate(DILATED_GROUPS):
        kb, vb = kv_blocks[2 * g], kv_blocks[2 * g + 1]
        qb = _to_residue_blocks(q[:, :, g], dil)
        bias, valid = _group_bias_mask(rel_bias, g, dil, qb.shape[2])
        s = jnp.einsum('brnqhe,brnkhe->brnhqk', qb, kb) + bias
        s = jnp.where(valid[:, None], s, NEG_BIG)
        lse = jax.nn.logsumexp(s, axis=-1)
        p = jnp.exp(s - lse[..., None])
        o = jnp.einsum('brnhqk,brnkhe->brnqhe', p, vb)
        outs.append(_from_residue_blocks(o, dil)[:, :seq])
        lses.append(_from_residue_blocks(jnp.swapaxes(lse, -1, -2), dil)[:, :seq])
    wts = jax.nn.softmax(jnp.stack(lses), axis=0)
    o = jnp.einsum('gblh,gblhe->blhe', wts, jnp.stack(outs))
    return o.reshape(bsz, seq, MERGED_WIDTH).astype(h.dtype) @ w_out


def setup_inputs(seed: int = 0) -> dict:
    key = jax.random.key(seed)
    ks = jax.random.split(key, 24)
    f32 = jnp.float32
    na, nbl, g, p, gs = N_A_LAYERS, N_B_LAYERS, SSM_GROUPS, SSM_STATE, SSM_GROUP

    def nrm(k, shape, scale):
        return jax.random.normal(k, shape, f32) * scale

    x = nrm(ks[0], (BATCH, SEQ, D_MODEL), 1.0)
    s5_lam_re = -0.5 + nrm(ks[1], (na, g, p), 0.01)
    s5_lam_im = math.pi * jnp.arange(p, dtype=f32) + nrm(ks[2], (na, g, p), 0.01)
    s5_log_dt = jax.random.uniform(ks[3], (na, g), f32, math.log(DT_MIN), math.log(DT_MAX))
    s5_b_re = nrm(ks[4], (na, g, p, gs), (2.0 * gs) ** -0.5)
    s5_b_im = nrm(ks[5], (na, g, p, gs), (2.0 * gs) ** -0.5)
    s5_c_re = nrm(ks[6], (na, g, gs, p), p ** -0.5)
    s5_c_im = nrm(ks[7], (na, g, gs, p), p ** -0.5)
    s5_d = nrm(ks[8], (na, g, gs), 1.0)
    s5_w_glu = nrm(ks[9], (na, D_MODEL, D_MODEL), D_MODEL ** -0.5)
    s5_b_glu = nrm(ks[10], (na, D_MODEL), 0.01)
    s5_w_out = nrm(ks[11], (na, D_MODEL, D_MODEL), D_MODEL ** -0.5 * DN_BETA)
    w_k = nrm(ks[12], (D_MODEL, ATT_WIDTH), D_MODEL ** -0.5)
    w_v = nrm(ks[13], (D_MODEL, ATT_WIDTH), D_MODEL ** -0.5 * DN_BETA)
    attn_w_kv = jnp.concatenate([w_k, w_v], axis=1)
    attn_w_q = nrm(ks[14], (nbl, D_MODEL, ATT_WIDTH), D_MODEL ** -0.5)
    attn_w_out = nrm(ks[15], (nbl, MERGED_WIDTH, D_MODEL), MERGED_WIDTH ** -0.5 * DN_BETA)
    rel_bias = nrm(ks[16], (REL_BUCKETS, N_DIL * HEADS_PER_GROUP), 0.5)
    ffn_w_up = nrm(ks[17], (DEPTH, D_MODEL, 2 * D_FF), D_MODEL ** -0.5 * DN_BETA)
    ffn_conv_w = nrm(ks[18], (DEPTH, CONV_WIDTH, 2 * D_FF), CONV_WIDTH ** -0.5)
    ffn_conv_b = nrm(ks[19], (DEPTH, 2 * D_FF), 0.01)
    ffn_w_down = nrm(ks[20], (DEPTH, D_FF, D_MODEL), D_FF ** -0.5 * DN_BETA)
    ln_gain = 1.0 + nrm(ks[21], (DEPTH, 2, D_MODEL), 0.01)
    ln_bias = nrm(ks[22], (DEPTH, 2, D_MODEL), 0.01)
    return {'x': x, 's5_lam_re': s5_lam_re, 's5_lam_im': s5_lam_im, 's5_log_dt': s5_log_dt,
            's5_b_re': s5_b_re, 's5_b_im': s5_b_im, 's5_c_re': s5_c_re, 's5_c_im': s5_c_im,
            's5_d': s5_d, 's5_w_glu': s5_w_glu, 's5_b_glu': s5_b_glu, 's5_w_out': s5_w_out,
            'attn_w_kv': attn_w_kv, 'attn_w_q': attn_w_q, 'attn_w_out': attn_w_out,
            'rel_bias': rel_bias, 'ffn_w_up': ffn_w_up, 'ffn_conv_w': ffn_conv_w,
            'ffn_conv_b': ffn_conv_b, 'ffn_w_down': ffn_w_down, 'ln_gain': ln_gain, 'ln_bias': ln_bias}


def reference(x, s5_lam_re, s5_lam_im, s5_log_dt, s5_b_re, s5_b_im, s5_c_re, s5_c_im,
              s5_d, s5_w_glu, s5_b_glu, s5_w_out, attn_w_kv, attn_w_q, attn_w_out,
              rel_bias, ffn_w_up, ffn_conv_w, ffn_conv_b, ffn_w_down, ln_gain, ln_bias):
    h = x
    kv_blocks = None
    for layer in range(DEPTH):
        if layer < N_A_LAYERS:
            i = layer
            mix = s5_mixer(h, s5_lam_re[i], s5_lam_im[i], s5_log_dt[i], s5_b_re[i], s5_b_im[i],
                           s5_c_re[i], s5_c_im[i], s5_d[i], s5_w_glu[i], s5_b_glu[i], s5_w_out[i])
        else:
            j = layer - N_A_LAYERS
            mix = dilated_attention(h, attn_w_q[j], attn_w_out[j], rel_bias, kv_blocks)
        h = post_norm(h, mix, ln_gain[layer, 0], ln_bias[layer, 0])
        ffn = conv_glu_ffn(h, ffn_w_up[layer], ffn_conv_w[layer], ffn_conv_b[layer], ffn_w_down[layer])
        h = post_norm(h, ffn, ln_gain[layer, 1], ln_bias[layer, 1])
        if layer == N_A_LAYERS - 1:
            kv_blocks = shared_kv(h, attn_w_kv)
    return h
```

```python
import math
from contextlib import ExitStack
import numpy as np
import concourse.bass as bass
import concourse.mybir as mybir
from concourse.bass_utils import run_bass_kernel_spmd

F32 = mybir.dt.float32
BF16 = mybir.dt.bfloat16
AF = mybir.ActivationFunctionType
ALU = mybir.AluOpType

ENGS = ("pe", "act", "dve", "pool", "sp")
N_DMA_SEMS = 16
SAME_ENG_SYNC = True
SAME_ENG_MIN_DIST = 2

D = 1024
DT = 8
SEQ = 2048
NSEQ = 2
NTOK = NSEQ * SEQ
NB = 256
NBLK = NTOK // NB
BPS = SEQ // NB
DFF = 2816
FT = 44
FH = 22
ALPHA = (2.0 * 2) ** 0.25
LN_EPS = 1e-5


class Prog:
    def __init__(self, nc):
        self.nc = nc
        self.q = {e: [] for e in ENGS}
        self.cnt = {e: 0 for e in ENGS}
        self.seen = {e: {} for e in ENGS}
        self.state = {}
        self.dma_rr = {e: 0 for e in ENGS}
        self.dma_cnt = {}
        self.n_ops = 0

    def _deps(self, reads, writes):
        deps = set()
        for k in reads:
            st = self.state.get(k)
            if st is not None and st[0] is not None:
                deps.add(st[0])
        for k in writes:
            st = self.state.get(k)
            if st is not None:
                if st[0] is not None:
                    deps.add(st[0])
                deps.update(st[1])
        return deps

    def _waits(self, eng, deps):
        best = {}
        for (sk, v) in deps:
            if best.get(sk, 0) < v:
                best[sk] = v
        waits = []
        for sk, v in best.items():
            if sk == eng and (eng == "pe" or not SAME_ENG_SYNC):
                continue
            if sk == eng and self.cnt[eng] - v >= SAME_ENG_MIN_DIST - 1:
                continue
            if self.seen[eng].get(sk, 0) < v:
                self.seen[eng][sk] = v
                waits.append((sk, v))
        return waits

    def _commit(self, ticket, reads, writes):
        for k in reads:
            st = self.state.get(k)
            if st is None:
                self.state[k] = [None, [ticket]]
            else:
                st[1].append(ticket)
                if len(st[1]) > 48:
                    best = {}
                    for (sk, v) in st[1]:
                        if best.get(sk, 0) < v:
                            best[sk] = v
                    st[1] = list(best.items())
        for k in writes:
            self.state[k] = [ticket, []]

    def op(self, eng, fn, reads=(), writes=(), inc=True):
        deps = self._deps(reads, writes)
        waits = self._waits(eng, deps)
        ticket = (eng, self.cnt[eng] + 1)
        if inc:
            self.cnt[eng] += 1
        self.q[eng].append((waits, fn, (eng, 1) if inc else None))
        self._commit(ticket, reads, writes)
        self.n_ops += 1
        return ticket

    def dma(self, eng, fn, reads=(), writes=()):
        deps = self._deps(reads, writes)
        waits = self._waits(eng, deps)
        i = self.dma_rr[eng]
        self.dma_rr[eng] = (i + 1) % N_DMA_SEMS
        sk = ("dma", eng, i)
        v = self.dma_cnt.get(sk, 0) + 16
        self.dma_cnt[sk] = v
        ticket = (sk, v)
        if v > 16 and self.seen[eng].get(sk, 0) < v - 16:
            self.seen[eng][sk] = v - 16
            waits = waits + [(sk, v - 16)]
        self.q[eng].append((waits, fn, (sk, 16)))
        self._commit(ticket, reads, writes)
        self.n_ops += 1
        return ticket

    def wait_all(self, eng, keys):
        deps = set()
        for k in keys:
            st = self.state.get(k)
            if st is not None:
                if st[0] is not None:
                    deps.add(st[0])
                deps.update(st[1])
        waits = self._waits(eng, deps)
        self.q[eng].append((waits, None, None))

    @staticmethod
    def all_sem_keys():
        ks = list(ENGS)
        for e in ("sp", "pool", "act"):
            for i in range(N_DMA_SEMS):
                ks.append(("dma", e, i))
        return ks

    def check(self):
        if not hasattr(self, "simv"):
            self.simv = {}
        pos = {e: 0 for e in ENGS}
        while True:
            progress = False
            for e in ENGS:
                q = self.q[e]
                while pos[e] < len(q):
                    waits, fn, inc = q[pos[e]]
                    if all(self.simv.get(sk, 0) >= v for sk, v in waits):
                        if inc is not None:
                            self.simv[inc[0]] = self.simv.get(inc[0], 0) + inc[1]
                        pos[e] += 1
                        progress = True
                    else:
                        break
            if all(pos[e] == len(self.q[e]) for e in ENGS):
                return
            if not progress:
                msg = []
                for e in ENGS:
                    if pos[e] < len(self.q[e]):
                        waits = self.q[e][pos[e]][0]
                        msg.append("%s@%d waits %s (have %s)" % (e, pos[e], waits, [self.simv.get(sk, 0) for sk, _ in waits]))
                raise RuntimeError("DEADLOCK: " + "; ".join(msg))

    def emit(self, block, sems):
        self.check()
        qs = self.q
        self.q = {e: [] for e in ENGS}

        def run(engname):
            def body(h):
                for waits, fn, inc in qs[engname]:
                    for (sk, v) in waits:
                        h.wait_ge(sems[sk], v)
                    if fn is not None:
                        ins = fn(h)
                        if inc is not None:
                            ins.then_inc(sems[inc[0]], inc[1])
            return body
        block.tensor(run("pe"))
        block.scalar(run("act"))
        block.vector(run("dve"))
        block.gpsimd(run("pool"))
        block.sync(run("sp"))


def mm(P, out, lhsT, rhs, start, stop, reads, writes, inc=None):
    if inc is None:
        inc = stop
    return P.op("pe", lambda e: e.matmul(out, lhsT=lhsT, rhs=rhs, start=start, stop=stop),
                reads=reads, writes=writes, inc=inc)


def tp(P, out, in_, ident, reads, writes, inc=True):
    return P.op("pe", lambda e: e.transpose(out=out, in_=in_, identity=ident), reads=reads, writes=writes, inc=inc)


def act(P, out, in_, func, reads, writes, scale=None, bias=None):
    kw = {}
    if scale is not None:
        kw["scale"] = scale
    if bias is not None:
        kw["bias"] = bias
    return P.op("act", lambda e: e.activation(out=out, in_=in_, func=func, **kw), reads=reads, writes=writes)


def tt(P, eng, out, in0, in1, op, reads, writes):
    return P.op(eng, lambda e: e.tensor_tensor(out=out, in0=in0, in1=in1, op=op), reads=reads, writes=writes)


def stt(P, out, in0, scalar, in1, op0, op1, reads, writes):
    return P.op("dve", lambda e: e.scalar_tensor_tensor(out=out, in0=in0, scalar=scalar, in1=in1, op0=op0, op1=op1),
                reads=reads, writes=writes)


def ts(P, eng, out, in0, s1, s2, op0, op1, reads, writes):
    return P.op(eng, lambda e: e.tensor_scalar(out=out, in0=in0, scalar1=s1, scalar2=s2, op0=op0, op1=op1),
                reads=reads, writes=writes)


def cp(P, eng, out, in_, reads, writes):
    if eng == "act":
        return act(P, out, in_, AF.Copy, reads, writes)
    return P.op(eng, lambda e: e.tensor_copy(out=out, in_=in_), reads=reads, writes=writes)


def memset(P, eng, ap, val, writes):
    return P.op(eng, lambda e: e.memset(ap, val), writes=writes)


def dma(P, out, in_, reads, writes, eng="sp"):
    return P.dma(eng, lambda e: e.dma_start(out=out, in_=in_), reads=reads, writes=writes)


class Ctx:
    pass


DBG = {}


def run_phase(C, fn):
    C.phase_idx = getattr(C, "phase_idx", 0) + 1
    pfx = "p%d_" % C.phase_idx
    with ExitStack() as es:
        def sb(name, shape, dt):
            return es.enter_context(C.nc.sbuf_tensor(pfx + name, shape, dt))
        fn(C, sb)
        with C.nc.Block() as block:
            C.P.emit(block, C.sems)


def load_cols(C, sb_tmp, dst, src2d, n, key_dst):
    P = C.P
    dma(P, sb_tmp[0:n, :], src2d, reads=[], writes=["lc_tmp"])
    tp(P, C.ps[7][:, 0:n], sb_tmp[0:n, :], C.ident[0:n, 0:n], reads=["lc_tmp", "ident"], writes=["ps7"])
    cp(P, "dve", dst, C.ps[7][:, 0:n], reads=["ps7"], writes=[key_dst])


def phase_setup(C, sb):
    P = C.P
    tmp = sb("lc_tmp", [128, 128], F32)
    dma(P, C.ident[:], C.d["ident"], reads=[], writes=["ident"])
    memset(P, "pool", C.onesb[:], 1.0 / 1024.0, writes=["onesb"])
    for l in range(2):
        for k in range(3):
            load_cols(C, tmp, C.cw[:, l, k, :], C.d["ffn_conv_w"][l, k].rearrange("(a b) -> a b", b=128), FT, "cw")
        load_cols(C, tmp, C.cb[:, l, :], C.d["ffn_conv_b"][l].rearrange("(a b) -> a b", b=128), FT, "cb")
    load_cols(C, tmp, C.lng[:], C.d["ln_gain"].rearrange("l i (a b) -> (l i a) b", b=128), 32, "lng")
    load_cols(C, tmp, C.lnb[:], C.d["ln_bias"].rearrange("l i (a b) -> (l i a) b", b=128), 32, "lnb")
    load_cols(C, tmp, C.bglu[:], C.d["s5_b_glu"][0].rearrange("(a b) -> a b", b=128), 8, "bglu")


def ln_gen(C, sb_t, z, zkeys, hout, hkeys, ln_idx, NB=NB):
    P = C.P
    mean_ps, ex2_ps = C.ps[5], C.ps[6]
    for dt in range(DT):
        zb = sb_t["zb"][dt % 2]
        zq = sb_t["zq"][dt % 2]
        kb, kq = "zb%d" % (dt % 2), "zq%d" % (dt % 2)
        cp(P, "act", zb[:], z[:, dt, :], reads=[zkeys[dt]], writes=[kb])
        tt(P, "pool", zq[:], z[:, dt, :], z[:, dt, :], ALU.mult, reads=[zkeys[dt]], writes=[kq])
        mm(P, mean_ps[:, 0:NB], C.onesb[:], zb[:], dt == 0, dt == DT - 1, reads=[kb, "onesb"], writes=["ps5"], inc=True)
        mm(P, ex2_ps[:, 0:NB], C.onesb[:], zq[:], dt == 0, dt == DT - 1, reads=[kq, "onesb"], writes=["ps6"], inc=True)
        yield
    mean = sb_t["mean"]
    m2 = sb_t["m2"]
    rstd = sb_t["rstd"]
    nmr = sb_t["nmr"]
    cp(P, "act", mean[:], mean_ps[:, 0:NB], reads=["ps5"], writes=["mean"])
    tt(P, "pool", m2[:], mean[:], mean[:], ALU.mult, reads=["mean"], writes=["m2"])
    tt(P, "dve", m2[:], ex2_ps[:, 0:NB], m2[:], ALU.subtract, reads=["ps6", "m2"], writes=["m2"])
    act(P, rstd[:], m2[:], AF.Sqrt, reads=["m2", "epsc"], writes=["rstd"], bias=C.epsc[:, 0:1])
    P.op("dve", lambda e: e.reciprocal(out=rstd[:], in_=rstd[:]), reads=["rstd"], writes=["rstd"])
    stt(P, nmr[:], mean[:], -1.0, rstd[:], ALU.mult, ALU.mult, reads=["mean", "rstd"], writes=["nmr"])
    yield
    for dt in range(DT):
        y = sb_t["y"][dt % 2]
        ky = "y%d" % (dt % 2)
        tt(P, "dve", y[:], z[:, dt, :], rstd[:], ALU.mult, reads=[zkeys[dt], "rstd"], writes=[ky])
        tt(P, "pool", y[:], y[:], nmr[:], ALU.add, reads=[ky, "nmr"], writes=[ky])
        act(P, hout[:, dt, :], y[:], AF.Identity, reads=[ky, "lng", "lnb"], writes=[hkeys[dt]],
            scale=C.lng[:, ln_idx * 8 + dt: ln_idx * 8 + dt + 1], bias=C.lnb[:, ln_idx * 8 + dt: ln_idx * 8 + dt + 1])
        yield


def layer_norm_block(C, sb_t, z, zkeys, hout, hkeys, ln_idx, NB=NB):
    for _ in ln_gen(C, sb_t, z, zkeys, hout, hkeys, ln_idx, NB=NB):
        pass


def ln_tiles(sb, NB=NB):
    t = {}
    t["zb"] = [sb("zb%d" % i, [128, NB], BF16) for i in range(2)]
    t["zq"] = [sb("zq%d" % i, [128, NB], BF16) for i in range(2)]
    t["y"] = [sb("y%d" % i, [128, NB], F32) for i in range(2)]
    for n in ("mean", "m2", "rstd", "nmr"):
        t[n] = sb(n, [128, NB], F32)
    return t


def make_phase_ffn(layer, src, dst):
    def phase(C, sb):
        P = C.P
        inT, outT = C.d[src], C.d[dst]
        wup = sb("wup", [128, DT, 2 * DFF], BF16)
        wdn = sb("wdn", [128, FH, D], BF16)
        hin = [sb("hin%d" % i, [128, DT, NB], F32) for i in range(2)]
        hbs = [sb("hb%d" % i, [128, DT, NB], BF16) for i in range(2)]
        NHC = 4
        hc = [sb("hc%d" % i, [128, NB + 2], F32) for i in range(NHC)]
        tb = [sb("tb%d" % i, [128, NB], F32) for i in range(4)]
        sl = [sb("sl%d" % i, [128, NB], F32) for i in range(2)]
        actts = [sb("actt%d" % i, [128, FH, NB], BF16) for i in range(2)]
        halo = sb("halo", [128, FT, 2], F32)
        hout = sb("hout", [128, DT, NB], F32)
        lt = ln_tiles(sb)
        wu, wd = C.d["ffn_w_up"][layer], C.d["ffn_w_down"][layer]
        for dt in range(DT):
            for c0 in range(0, 2 * DFF, 1024):
                c1 = min(c0 + 1024, 2 * DFF)
                dma(P, wup[:, dt, c0:c1], wu[dt * 128:(dt + 1) * 128, c0:c1], reads=[], writes=[("wup", dt, c0)], eng="pool")
        for ft in range(FH):
            dma(P, wdn[:, ft, :], wd[ft * 128:(ft + 1) * 128, :], reads=[], writes=[("wdn", ft)], eng="pool")

        def wup_keys(dt, col):
            return ("wup", dt, (col // 1024) * 1024)

        UPB = (0, 1, 2, 7)

        def load_blk(b):
            h = hin[b % 2]
            dma(P, h[:], inT[:, :, b * NB:(b + 1) * NB].rearrange("d p t -> p d t"), reads=[(src, b)],
                writes=[("hin", b % 2, dt) for dt in range(DT)])
            for dt in range(DT):
                cp(P, "pool", hbs[b % 2][:, dt, :], h[:, dt, :], reads=[("hin", b % 2, dt)], writes=[("hb", b % 2, dt)])

        def up_pair(b, i):
            hb = hbs[b % 2]
            if i == 0 and b % BPS == 0:
                memset(P, "pool", halo[:], 0.0, writes=[("halo", ft) for ft in range(FT)])
            for which in range(2):
                ft = i + which * FH
                j = i * 2 + which
                pb = C.ps[UPB[j % 4]]
                pk = "ps%d" % UPB[j % 4]
                for dt in range(DT):
                    mm(P, pb[:, 0:NB], wup[:, dt, ft * 128:(ft + 1) * 128], hb[:, dt, :], dt == 0, dt == DT - 1,
                       reads=[wup_keys(dt, ft * 128), ("hb", b % 2, dt)], writes=[pk])
                hcb = hc[j % NHC]
                hck = "hc%d" % (j % NHC)
                hhk = "hch%d" % (j % NHC)
                cp(P, "pool", hcb[:, 0:2], halo[:, ft, :], reads=[("halo", ft)], writes=[hhk])
                cp(P, "act", hcb[:, 2:NB + 2], pb[:, 0:NB], reads=[pk], writes=[hck])
                cp(P, "pool", halo[:, ft, :], hcb[:, NB:NB + 2], reads=[hck], writes=[("halo", ft)])
                t = tb[j % 4]
                tk = "tb%d" % (j % 4)
                act(P, t[:], hcb[:, 2:NB + 2], AF.Identity, reads=[hck, "cw", "cb"], writes=[tk],
                    scale=C.cw[:, layer, 0, ft:ft + 1], bias=C.cb[:, layer, ft:ft + 1])
                stt(P, t[:], hcb[:, 1:NB + 1], C.cw[:, layer, 1, ft:ft + 1], t[:], ALU.mult, ALU.add,
                    reads=[hck, hhk, tk, "cw"], writes=[tk])
                stt(P, t[:], hcb[:, 0:NB], C.cw[:, layer, 2, ft:ft + 1], t[:], ALU.mult, ALU.add,
                    reads=[hck, hhk, tk, "cw"], writes=[tk])
                if which == 1:
                    s = sl[i % 2]
                    sk = "sl%d" % (i % 2)
                    act(P, s[:], t[:], AF.Silu, reads=[tk], writes=[sk])
                    tv = tb[(j - 1) % 4]
                    tvk = "tb%d" % ((j - 1) % 4)
                    tt(P, "dve", actts[b % 2][:, i, :], s[:], tv[:], ALU.mult, reads=[sk, tvk], writes=[("actt", b % 2, i)])

        def down_dt(b, dt):
            h = hin[b % 2]
            hkd = ("hin", b % 2, dt)
            pb = C.ps[3 + dt % 2]
            pk = "ps%d" % (3 + dt % 2)
            for i in range(FH):
                mm(P, pb[:, 0:NB], wdn[:, i, dt * 128:(dt + 1) * 128], actts[b % 2][:, i, :], i == 0, i == FH - 1,
                   reads=[("wdn", i), ("actt", b % 2, i)], writes=[pk])
            stt(P, h[:, dt, :], h[:, dt, :], ALPHA, pb[:, 0:NB], ALU.mult, ALU.add, reads=[hkd, pk], writes=[hkd])

        KA, KB = 8, 9
        nblk = DBG.get("nblk", NBLK)
        load_blk(0)
        for i in range(FH):
            up_pair(0, i)
        for b in range(nblk):
            h = hin[b % 2]
            hk = [("hin", b % 2, dt) for dt in range(DT)]
            nxt = b + 1 < nblk
            if nxt:
                load_blk(b + 1)
            for dt in range(DT):
                down_dt(b, dt)
                if nxt and dt < KA:
                    up_pair(b + 1, dt)
            houtk = [("hout", dt) for dt in range(DT)]
            g = ln_gen(C, lt, h, hk, hout, houtk, layer * 2 + 1)
            if nxt:
                for i in range(KA, KA + KB):
                    up_pair(b + 1, i)
                    next(g, None)
                    next(g, None)
            for _ in g:
                pass
            dma(P, outT[:, :, b * NB:(b + 1) * NB].rearrange("d p t -> p d t"), hout[:], reads=houtk, writes=[(dst, b)])
            if nxt:
                for i in range(KA + KB, FH):
                    up_pair(b + 1, i)
    return phase


WEIGHT_NAMES = ["s5_lam_re", "s5_lam_im", "s5_log_dt", "s5_b_re", "s5_b_im", "s5_c_re", "s5_c_im", "s5_d",
                "s5_w_glu", "s5_b_glu", "s5_w_out", "attn_w_kv", "attn_w_q", "attn_w_out", "rel_bias",
                "ffn_w_up", "ffn_conv_w", "ffn_conv_b", "ffn_w_down", "ln_gain", "ln_bias"]
WEIGHT_SHAPES = {
    "s5_lam_re": (1, 64, 64), "s5_lam_im": (1, 64, 64), "s5_log_dt": (1, 64), "s5_b_re": (1, 64, 64, 16),
    "s5_b_im": (1, 64, 64, 16), "s5_c_re": (1, 64, 16, 64), "s5_c_im": (1, 64, 16, 64), "s5_d": (1, 64, 16),
    "s5_w_glu": (1, 1024, 1024), "s5_b_glu": (1, 1024), "s5_w_out": (1, 1024, 1024), "attn_w_kv": (1024, 6144),
    "attn_w_q": (1, 1024, 3072), "attn_w_out": (1, 1024, 1024), "rel_bias": (32, 48), "ffn_w_up": (2, 1024, 5632),
    "ffn_conv_w": (2, 3, 5632), "ffn_conv_b": (2, 5632), "ffn_w_down": (2, 2816, 1024), "ln_gain": (2, 2, 1024),
    "ln_bias": (2, 2, 1024)}


def build_program(phases, ext_in=(), ext_out=()):
    nc = bass.Bass("TRN2", target_bir_lowering=False)
    C = Ctx()
    C.nc = nc
    C.P = Prog(nc)
    C.d = {}
    for n in WEIGHT_NAMES:
        C.d[n] = nc.dram_tensor(n, list(WEIGHT_SHAPES[n]), F32, kind="ExternalInput").ap()
    C.d["ident"] = nc.dram_tensor("ident", [128, 128], F32, kind="ExternalInput").ap()
    C.d["antiI"] = nc.dram_tensor("antiI", [128, 128], F32, kind="ExternalInput").ap()
    C.d["onehot"] = nc.dram_tensor("onehot", [33, 3, 384], F32, kind="ExternalInput").ap()
    C.d_handles = {}
    C.d_handles["ebf"] = nc.dram_tensor("ebf", [48, 384], F32, kind="Internal")
    C.d["ebf"] = C.d_handles["ebf"].ap()
    if DBG.get("s5dump"):
        C.d["dbg"] = nc.dram_tensor("dbg", [8, 128, 4096], F32, kind="ExternalOutput").ap()
    acts = ["xT", "haT", "h1T", "hcT", "outT"]
    for n in acts:
        kind = "Internal"
        if n in ext_in or n == "xT":
            kind = "ExternalInput"
        if n in ext_out or n == "outT":
            kind = "ExternalOutput"
        C.d[n] = nc.dram_tensor(n, [DT, 128, NTOK], F32, kind=kind).ap()
    with ExitStack() as es:
        def sbg(name, shape, dt):
            return es.enter_context(nc.sbuf_tensor(name, shape, dt))
        C.ident = sbg("ident_sb", [128, 128], F32)
        C.onesb = sbg("onesb", [128, 128], BF16)
        C.cw = sbg("cw", [128, 2, 3, FT], F32)
        C.cb = sbg("cb", [128, 2, FT], F32)
        C.lng = sbg("lng", [128, 32], F32)
        C.lnb = sbg("lnb", [128, 32], F32)
        C.bglu = sbg("bglu", [128, 8], F32)
        C.epsc = sbg("epsc", [128, 1], F32)
        C.ps = [es.enter_context(nc.psum_tensor("psb%d" % i, [128, 512], F32)) for i in range(8)]
        C.sems = {}
        for k in Prog.all_sem_keys():
            nm = "s_" + "_".join(str(x) for x in (k if isinstance(k, tuple) else (k,)))
            C.sems[k] = es.enter_context(nc.semaphore(nm))
        memset(C.P, "dve", C.epsc[:], LN_EPS, writes=["epsc"])
        table = {
            "setup": phase_setup,
            "ffn0": make_phase_ffn(0, "haT", "h1T"),
            "ffn1": make_phase_ffn(1, "hcT", "outT"),
            "attn": phase_attn,
            "s5": phase_s5,
        }
        for ph in phases:
            run_phase(C, table[ph])
        def fin(C, sb):
            C.P.wait_all("sp", [(n, b) for n in ("outT", "haT", "h1T", "hcT") for b in range(NBLK)] + ["dbg%d" % i for i in range(8)])
        run_phase(C, fin)
    return nc


def host_consts():
    return {"ident": np.eye(128, dtype=np.float32), "onehot": host_onehot(),
            "antiI": np.ascontiguousarray(np.eye(128, dtype=np.float32)[::-1])}


def kernel(**inputs):
    n = 8
    x = np.ascontiguousarray(inputs["x"], dtype=np.float32)
    nc = build_program(["setup", "s5", "ffn0", "attn", "ffn1"])
    consts = host_consts()
    in_maps = []
    for c in range(n):
        xs = x[c * NSEQ:(c + 1) * NSEQ].reshape(NTOK, DT, 128)
        m = {k: np.ascontiguousarray(inputs[k], dtype=np.float32) for k in WEIGHT_NAMES}
        m.update(consts)
        m["xT"] = np.ascontiguousarray(xs.transpose(1, 2, 0))
        in_maps.append(m)
    res = run_bass_kernel_spmd(nc, in_maps, core_ids=list(range(n)))
    outs = []
    for c in range(n):
        o = res.results[c]["outT"]
        outs.append(o.transpose(2, 0, 1).reshape(NSEQ, SEQ, D))
    return np.ascontiguousarray(np.concatenate(outs, axis=0), dtype=np.float32)


DILS = (1, 4, 16)
REL_BUCKETS = 32
REL_MAX_DIST = 2048


def _t5_bucket(dist):
    exact = REL_BUCKETS // 2
    d = np.maximum(dist, 1).astype(np.float32)
    large = exact + (np.log(d / exact) / math.log(REL_MAX_DIST / exact) * (REL_BUCKETS - exact)).astype(np.int64)
    large = np.minimum(large, REL_BUCKETS - 1)
    return np.where(dist < exact, dist, large).astype(np.int32)


def host_onehot():
    oh = np.zeros((33, 3, 384), np.float32)
    for g, dil in enumerate(DILS):
        for dp in range(384):
            delta = dp - 127
            if 0 <= delta <= 128:
                b = int(_t5_bucket(np.array([delta * dil]))[0])
                oh[b, g, dp] = 1.0
            else:
                oh[32, g, dp] = 1.0
    return oh


def phase_attn(C, sb):
    P = C.P
    h1T, hcT = C.d["h1T"], C.d["hcT"]
    wq_d = C.d["attn_w_q"][0].rearrange("(dt p) c -> p dt c", p=128)
    wkv_d = C.d["attn_w_kv"].rearrange("(dt p) c -> p dt c", p=128)
    wo_d = C.d["attn_w_out"][0].rearrange("(hp p) c -> p hp c", p=128)
    ebf_t = C.d_handles["ebf"]
    ebf = C.d["ebf"]
    hT = sb("hT", [128, DT, SEQ], BF16)
    OT = sb("OT", [128, 8, SEQ], BF16)
    NUM1 = sb("NUM", [128, SEQ], F32)
    DEN1 = sb("DEN", [128, SEQ], F32)
    NUM = [NUM1, NUM1]
    DEN = [DEN1, DEN1]
    QTz = [[sb("QTz%d_%d" % (i, h), [128, SEQ], BF16) for h in range(2)] for i in range(2)]
    KT = [sb("KT%d" % i, [128, SEQ], BF16) for i in range(2)]
    Vz = [[sb("Vz%d_%d" % (i, h), [128, 16, 128], BF16) for h in range(2)] for i in range(2)]
    wq = [sb("wq%d" % i, [128, DT, 128], BF16) for i in range(2)]
    wk = [sb("wk%d" % i, [128, DT, 128], BF16) for i in range(2)]
    wv = [sb("wv%d" % i, [128, DT, 128], BF16) for i in range(2)]
    wo = sb("wo", [128, 8, D], BF16)
    ebt = [sb("ebt%d" % i, [128, 4, 128], F32) for i in range(2)]
    ebu = [sb("ebu%d" % i, [128, 4, 128], F32) for i in range(2)]
    Eb = [sb("Eb%d" % i, [128, 512], BF16) for i in range(2)]
    Pm = [sb("Pm%d" % i, [128, 512], BF16) for i in range(2)]
    onesz = [sb("onesz%d" % i, [128, 128], BF16) for i in range(2)]
    antiI = sb("antiI_sb", [128, 128], F32)
    rb33 = sb("rb33", [33, 48], F32)
    oh = sb("oh", [33, 3, 384], F32)
    fe = sb("fe", [16, 384], F32)
    hres = sb("hres", [128, DT, NB], F32)
    hout = sb("hout", [128, DT, NB], F32)
    lt = ln_tiles(sb)
    dma(P, antiI[:], C.d["antiI"], reads=[], writes=["antiI"])
    for i in range(2):
        for h in range(2):
            memset(P, "pool", Vz[i][h][:], 0.0, writes=[("Vz", i, h)])
            memset(P, "pool", QTz[i][h][:], 0.0, writes=[("QT", i)])
        memset(P, "pool", onesz[i][:], 0.0, writes=[("onesz", i)])
        memset(P, "pool", onesz[i][:, i * 64:(i + 1) * 64], 1.0, writes=[("onesz", i)])
    for hp in range(8):
        dma(P, wo[:, hp, :], wo_d[:, hp, :], reads=[], writes=[("wo", hp)], eng="pool")
    memset(P, "dve", rb33[:], -30000.0, writes=["rb33"])
    dma(P, rb33[0:32, :], C.d["rel_bias"], reads=[], writes=["rb33"])
    dma(P, oh[:], C.d["onehot"], reads=[], writes=["oh"])
    for g in range(3):
        mm(P, C.ps[0][0:16, 0:384], rb33[:, g * 16:(g + 1) * 16], oh[:, g, :], True, True, reads=["rb33", "oh"], writes=["ps0"])
        act(P, fe[:], C.ps[0][0:16, 0:384], AF.Exp, reads=["ps0"], writes=["fe"])
        dma(P, ebf[g * 16:(g + 1) * 16, :], fe[:], reads=["fe"], writes=["ebf"])

    iters = [(s, hp, g) for s in range(NSEQ) for hp in range(8) for g in range(3)][:DBG.get("niter", 48)]

    def proj_units(i):
        s, hp, g = iters[i]
        dil = DILS[g]
        nbc = 16 // dil
        pi = i % 2
        col = g * 1024 + hp * 128
        units = []

        def u0():
            if hp == 0 and g == 0:
                t0 = s * SEQ
                for dt in range(DT):
                    dma(P, hT[:, dt, :], h1T[dt, :, t0:t0 + SEQ], reads=[("h1T", b) for b in range(s * BPS, (s + 1) * BPS)],
                        writes=[("hT", dt)], eng="pool")
            dma(P, wq[pi][:], wq_d[:, :, col:col + 128], reads=[], writes=[("wq", pi)], eng="pool")
            dma(P, wk[pi][:], wkv_d[:, :, col:col + 128], reads=[], writes=[("wk", pi)], eng="pool")
            dma(P, wv[pi][:], wkv_d[:, :, 3072 + col:3072 + col + 128], reads=[], writes=[("wv", pi)], eng="pool")
            for h2 in range(2):
                head = g * 16 + hp * 2 + h2
                for pc in range(2):
                    off = head * 384 + (128 if pc == 0 else 0)
                    src = bass.AP(tensor=ebf_t, offset=off, ap=[[1, 128], [1, 128]])
                    dma(P, ebu[pi][:, h2 * 2 + pc, :], src, reads=["ebf"], writes=[("ebu", pi)])
            mm(P, C.ps[5][:, 0:512], antiI[:], ebu[pi][:].rearrange("p a b -> p (a b)"), True, True,
               reads=[("ebu", pi), "antiI"], writes=["ps5"])
            cp(P, "dve", ebt[pi][:].rearrange("p a b -> p (a b)"), C.ps[5][:, 0:512], reads=["ps5"], writes=[("ebt", pi)])
        units.append(u0)
        mpc = 512 // dil
        for which in range(2):
            for tb in range(4):
                def uqk(which=which, tb=tb):
                    w = (wq if which == 0 else wk)[pi]
                    wkey = ("wq" if which == 0 else "wk", pi)
                    key = ("QT" if which == 0 else "KT", pi)
                    pb = C.ps[tb % 2]
                    pk = "ps%d" % (tb % 2)
                    for dt in range(DT):
                        mm(P, pb[:, 0:512], w[:, dt, :], hT[:, dt, tb * 512:(tb + 1) * 512], dt == 0, dt == DT - 1,
                           reads=[wkey, ("hT", dt)], writes=[pk])
                    if which == 0:
                        for h2 in range(2):
                            src = pb[h2 * 64:(h2 + 1) * 64, 0:512].rearrange("p (m r) -> p r m", r=dil)
                            dst = QTz[pi][h2][h2 * 64:(h2 + 1) * 64, :].rearrange("p (r m) -> p r m", r=dil)[:, :, tb * mpc:(tb + 1) * mpc]
                            act(P, dst, src, AF.Copy, reads=[pk], writes=[key], scale=0.125)
                    else:
                        src = pb[:, 0:512].rearrange("p (m r) -> p r m", r=dil)
                        dst = KT[pi][:].rearrange("p (r m) -> p r m", r=dil)[:, :, tb * mpc:(tb + 1) * mpc]
                        cp(P, "dve", dst, src, reads=[pk], writes=[key])
                units.append(uqk)
        for kb in range(16):
            def uv(kb=kb):
                r, n = kb // nbc, kb % nbc
                start = n * 128 * dil + r
                pb = C.ps[2]
                sub = kb % 4
                for dt in range(DT):
                    if dil == 1:
                        lhs = hT[:, dt, start:start + 128]
                    else:
                        lhs = hT[:, dt, start - r:start - r + 128 * dil].rearrange("p (m r) -> p r m", r=dil)[:, r, :]
                    mm(P, pb[:, sub * 128:(sub + 1) * 128], lhs, wv[pi][:, dt, :], dt == 0, dt == DT - 1,
                       reads=[("wv", pi), ("hT", dt)], writes=["ps2"])
                cp(P, DBG.get("veng", "act"), Vz[pi][0][:, kb, 0:64], pb[:, sub * 128:sub * 128 + 64], reads=["ps2"], writes=[("Vz", pi, 0)])
                cp(P, DBG.get("veng", "act"), Vz[pi][1][:, kb, 64:128], pb[:, sub * 128 + 64:(sub + 1) * 128], reads=["ps2"], writes=[("Vz", pi, 1)])
            units.append(uv)
        return units

    def scores(i, qb):
        s, hp, g = iters[i]
        dil = DILS[g]
        nbc = 16 // dil
        pi = i % 2
        n = qb % nbc
        pcs = [1] if n == 0 else [0, 1]
        e, ek = Eb[qb % 2], "Eb%d" % (qb % 2)
        pm, pmk = Pm[qb % 2], "Pm%d" % (qb % 2)
        eb, ebk = ebt[pi], ("ebt", pi)
        sps = C.ps[3 + qb % 2]
        spk = "ps%d" % (3 + qb % 2)
        last = (1, 1)
        for h2 in range(2):
            for pc in pcs:
                kb = qb if pc == 1 else qb - 1
                slot = h2 * 2 + pc
                mm(P, sps[:, slot * 128:(slot + 1) * 128], KT[pi][:, kb * 128:(kb + 1) * 128],
                   QTz[pi][h2][:, qb * 128:(qb + 1) * 128], True, True,
                   reads=[("QT", pi), ("KT", pi)], writes=[spk], inc=((h2, pc) == last))
        if n == 0:
            for h2 in range(2):
                c0 = h2 * 256 + 128
                act(P, e[:, c0:c0 + 128], sps[:, c0:c0 + 128], AF.Exp, reads=[spk], writes=[ek])
        else:
            act(P, e[:], sps[:, 0:512], AF.Exp, reads=[spk], writes=[ek])
        if n == 0:
            for h2 in range(2):
                tt(P, "dve", pm[:, h2 * 256 + 128:(h2 + 1) * 256], e[:, h2 * 256 + 128:(h2 + 1) * 256], eb[:, h2 * 2 + 1, :],
                   ALU.mult, reads=[ek, ebk], writes=[pmk])
        else:
            tt(P, "dve", pm[:], e[:], eb[:].rearrange("p a b -> p (a b)"), ALU.mult, reads=[ek, ebk], writes=[pmk])

    def pv(i, qb):
        s, hp, g = iters[i]
        dil = DILS[g]
        nbc = 16 // dil
        pi = i % 2
        r, n = qb // nbc, qb % nbc
        start = n * 128 * dil + r
        pcs = [1] if n == 0 else [0, 1]
        pm, pmk = Pm[qb % 2], "Pm%d" % (qb % 2)
        nd = C.ps[7 - qb % 2][:, 0:256]
        ndk = "ps%d" % (7 - qb % 2)
        combos = [(h2, pc) for h2 in range(2) for pc in pcs]
        for part in range(2):
            for ci, (h2, pc) in enumerate(combos):
                kb = qb if pc == 1 else qb - 1
                slot = h2 * 2 + pc
                lhs = Vz[pi][h2][:, kb, :] if part == 0 else onesz[h2][:]
                mm(P, nd[:, part * 128:(part + 1) * 128], lhs, pm[:, slot * 128:(slot + 1) * 128],
                   ci == 0, ci == len(combos) - 1,
                   reads=[pmk, ("Vz", pi, h2), ("onesz", h2)], writes=[ndk], inc=(part == 1 and ci == len(combos) - 1))
        nb = (s * 8 + hp) % 2
        nk, dk = ("NUM", nb), ("DEN", nb)
        if dil > 1:
            numv = NUM[nb][:, start - r:start - r + 128 * dil].rearrange("p (m r) -> p r m", r=dil)[:, r, :]
            denv = DEN[nb][:, start - r:start - r + 128 * dil].rearrange("p (m r) -> p r m", r=dil)[:, r, :]
        else:
            numv = NUM[nb][:, start:start + 128]
            denv = DEN[nb][:, start:start + 128]
        if g == 0:
            cp(P, "dve", numv, nd[:, 0:128], reads=[ndk], writes=[nk])
            cp(P, "dve", denv, nd[:, 128:256], reads=[ndk], writes=[dk])
        else:
            tt(P, "dve", numv, nd[:, 0:128], numv, ALU.add, reads=[ndk, nk], writes=[nk])
            tt(P, "dve", denv, nd[:, 128:256], denv, ALU.add, reads=[ndk, dk], writes=[dk])

    def out_proj(s):
        for bl in range(BPS):
            b = s * BPS + bl
            hk = [("hres", dt) for dt in range(DT)]
            dma(P, hres[:], h1T[:, :, b * NB:(b + 1) * NB].rearrange("d p t -> p d t"), reads=[("h1T", b)], writes=hk)
            for dt in range(DT):
                pb = C.ps[dt % 2]
                pk = "ps%d" % (dt % 2)
                for hp in range(8):
                    mm(P, pb[:, 0:NB], wo[:, hp, dt * 128:(dt + 1) * 128], OT[:, hp, bl * NB:(bl + 1) * NB], hp == 0, hp == 7,
                       reads=[("wo", hp), ("OT", hp)], writes=[pk])
                stt(P, hres[:, dt, :], hres[:, dt, :], ALPHA, pb[:, 0:NB], ALU.mult, ALU.add, reads=[hk[dt], pk], writes=[hk[dt]])
            houtk = [("hout", dt) for dt in range(DT)]
            layer_norm_block(C, lt, hres, hk, hout, houtk, 2)
            dma(P, hcT[:, :, b * NB:(b + 1) * NB].rearrange("d p t -> p d t"), hout[:], reads=houtk, writes=[("hcT", b)])

    for u in proj_units(0):
        u()
    for i in range(len(iters)):
        s, hp, g = iters[i]
        nxt = proj_units(i + 1) if i + 1 < len(iters) else []
        ui = 0
        for qb in range(16):
            scores(i, qb)
            if DBG.get("nopipe"):
                pv(i, qb)
            elif qb >= 1:
                pv(i, qb - 1)
            for _ in range(2):
                if ui < len(nxt):
                    nxt[ui]()
                    ui += 1
        if not DBG.get("nopipe"):
            pv(i, 15)
        while ui < len(nxt):
            nxt[ui]()
            ui += 1
        if g == 2:
            nb = (s * 8 + hp) % 2
            for q4 in range(4):
                sl = slice(q4 * 512, (q4 + 1) * 512)
                P.op("dve", lambda e, o=DEN[nb][:, sl]: e.reciprocal(out=o, in_=o), reads=[("DEN", nb)], writes=[("DEN", nb)])
                tt(P, "pool", OT[:, hp, sl], NUM[nb][:, sl], DEN[nb][:, sl], ALU.mult, reads=[("NUM", nb), ("DEN", nb)], writes=[("OT", hp)])
            if hp == 7:
                out_proj(s)


TC = 64
NCH = SEQ // TC


def phase_s5(C, sb):
    P = C.P
    W = 2 * TC
    xT, haT = C.d["xT"], C.d["haT"]
    tmp = sb("s5tmp", [128, 128], F32)
    sm = {n: sb("s5_" + n, [128, 32], F32) for n in
          ("lr", "li", "ldt", "dt", "wr", "wi", "ar", "ai", "t1", "t2", "t3", "t4", "er", "ei", "den", "cor", "coi", "nr")}
    scr = sb("s5scr", [128, 6144], F32)

    def v16(j):
        return scr[:, 2048 + j * 512:2048 + (j + 1) * 512].rearrange("p (g h) -> p g h", h=16)
    Bn = [v16(0), v16(1)]
    Bb = [v16(2), v16(3)]
    bt = [v16(4), v16(5)]
    Z = sb("Zs5", [128, 128], F32)
    Bblk = [sb("Bblk%d" % i, [128, 32, 128], BF16) for i in range(2)]
    Cblk = [sb("Cblk%d" % i, [128, 32, 128], BF16) for i in range(2)]
    xcb = [sb("xcb%d" % i, [128, DT, W], BF16) for i in range(2)]
    hb16 = [sb("hb16_%d" % i, [128, 32, W], BF16) for i in range(2)]
    Cnat1 = scr[0:16, 0:2048].rearrange("p (a b) -> p a b", b=64)
    Cnat = [Cnat1, Cnat1]
    dcol = sb("dcol", [128, 8], F32)
    wglu = sb("wglu", [128, DT, D], BF16)
    wout = sb("wout", [128, DT, D], BF16)
    xc = [sb("xc%d" % i, [128, DT, W], F32) for i in range(2)]
    BU1 = [sb("BU_%d" % j, [128, 32, W], F32) for j in range(2)]
    cosT = sb("s5cos", [128, 32, TC], F32)
    sinT = sb("s5sin", [128, 32, TC], F32)
    rT = sb("s5rT", [128, 32, TC], F32)
    T1 = scr[:, 0:2048].rearrange("p (g t) -> p g t", t=TC)
    T2 = scr[:, 2048:4096].rearrange("p (g t) -> p g t", t=TC)
    T3 = scr[:, 4096:6144].rearrange("p (g t) -> p g t", t=TC)
    hprev = [[sb("s5hp%d_%d" % (s_, j), [128, 32], F32) for j in range(2)] for s_ in range(2)]
    BU = [BU1, BU1]
    ys = sb("ys", [128, DT, W], F32)
    ygf = sb("ygf", [128, DT, W], F32)
    ygb = sb("ygb", [128, DT, W], BF16)
    sig = [sb("sig%d" % i, [128, W], F32) for i in range(2)]
    gb = sb("gb", [128, DT, W], BF16)
    hout = sb("s5hout", [128, DT, W], F32)
    lt = ln_tiles(lambda n, sh, d: sb("s5" + n, sh, d), NB=W)
    k = lambda n: "s5_" + n

    def v(op, out, a, b, eng="dve"):
        tt(P, eng, sm[out][:], sm[a][:], sm[b][:], op, reads=[k(a), k(b)], writes=[k(out)])

    def vs(out, a, s1, s2=None, op0=ALU.mult, op1=ALU.add):
        ts(P, "dve", sm[out][:], sm[a][:], s1, s2, op0, op1 if s2 is not None else ALU.bypass, reads=[k(a)], writes=[k(out)])

    load_cols(C, tmp, sm["lr"][:], C.d["s5_lam_re"][0].rearrange("(gp g2) p -> gp (g2 p)", g2=2), 32, k("lr"))
    load_cols(C, tmp, sm["li"][:], C.d["s5_lam_im"][0].rearrange("(gp g2) p -> gp (g2 p)", g2=2), 32, k("li"))
    ld2 = sb("ld2", [32, 2], F32)
    ldn = sb("ldn", [32, 128], F32)
    dma(P, ld2[:], C.d["s5_log_dt"][0].rearrange("(gp g2) -> gp g2", g2=2), reads=[], writes=["ld2"])
    for g2 in range(2):
        ts(P, "dve", ldn[:, g2 * 64:(g2 + 1) * 64], C.ident[0:32, 0:64], 0.0, ld2[:, g2:g2 + 1], ALU.mult, ALU.add,
           reads=["ident", "ld2"], writes=["ldn"])
    tp(P, C.ps[7][:, 0:32], ldn[:], C.ident[0:32, 0:32], reads=["ldn", "ident"], writes=["ps7"])
    cp(P, "dve", sm["ldt"][:], C.ps[7][:, 0:32], reads=["ps7"], writes=[k("ldt")])
    act(P, sm["dt"][:], sm["ldt"][:], AF.Exp, reads=[k("ldt")], writes=[k("dt")])
    v(ALU.mult, "wr", "lr", "dt"); vs("wr", "wr", 1.0 / 64)
    v(ALU.mult, "wi", "li", "dt"); vs("wi", "wi", 1.0 / 64)
    vs("ar", "wr", 1.0 / 8, 1.0)
    vs("ai", "wi", 1.0 / 8)
    for kk in range(7, 0, -1):
        v(ALU.mult, "t1", "ar", "wr"); v(ALU.mult, "t2", "ai", "wi"); v(ALU.subtract, "t1", "t1", "t2")
        v(ALU.mult, "t3", "ar", "wi"); v(ALU.mult, "t4", "ai", "wr"); v(ALU.add, "t3", "t3", "t4")
        vs("ar", "t1", 1.0 / kk, 1.0)
        vs("ai", "t3", 1.0 / kk)
    for _ in range(6):
        v(ALU.mult, "t1", "ar", "ar"); v(ALU.mult, "t2", "ai", "ai"); v(ALU.mult, "t3", "ar", "ai")
        v(ALU.subtract, "ar", "t1", "t2"); vs("ai", "t3", 2.0)
    v(ALU.mult, "t1", "lr", "dt")
    act(P, sm["er"][:], sm["t1"][:], AF.Exp, reads=[k("t1")], writes=[k("er")])
    P.op("dve", lambda e, o=sm["ei"][:], i_=sm["er"][:]: e.reciprocal(out=o, in_=i_), reads=[k("er")], writes=[k("ei")])
    v(ALU.mult, "t3", "ar", "ei")
    v(ALU.mult, "t4", "ai", "ei")
    cp(P, "dve", cosT[:, :, 0], sm["t3"][:], reads=[k("t3")], writes=["s5cos"])
    cp(P, "dve", sinT[:, :, 0], sm["t4"][:], reads=[k("t4")], writes=["s5sin"])
    memset(P, "dve", rT[:, :, 0], 0.0, writes=["s5rT"])
    for tau in range(1, TC):
        cp(P, "dve", rT[:, :, tau], sm["er"][:], reads=[k("er")], writes=["s5rT"])
        tt(P, "dve", sm["t1"][:], cosT[:, :, tau - 1], sm["t3"][:], ALU.mult, reads=["s5cos", k("t3")], writes=[k("t1")])
        tt(P, "dve", sm["t2"][:], sinT[:, :, tau - 1], sm["t4"][:], ALU.mult, reads=["s5sin", k("t4")], writes=[k("t2")])
        tt(P, "dve", cosT[:, :, tau], sm["t1"][:], sm["t2"][:], ALU.subtract, reads=[k("t1"), k("t2")], writes=["s5cos"])
        tt(P, "dve", sm["t1"][:], cosT[:, :, tau - 1], sm["t4"][:], ALU.mult, reads=["s5cos", k("t4")], writes=[k("t1")])
        tt(P, "dve", sm["t2"][:], sinT[:, :, tau - 1], sm["t3"][:], ALU.mult, reads=["s5sin", k("t3")], writes=[k("t2")])
        tt(P, "dve", sinT[:, :, tau], sm["t1"][:], sm["t2"][:], ALU.add, reads=[k("t1"), k("t2")], writes=["s5sin"])
    for s_ in range(2):
        for j in range(2):
            memset(P, "dve", hprev[s_][j][:], 0.0, writes=[("hp", s_, j)])
    vs("nr", "ar", 1.0, -1.0)
    v(ALU.mult, "t1", "lr", "lr"); v(ALU.mult, "t2", "li", "li"); v(ALU.add, "den", "t1", "t2")
    P.op("dve", lambda e, o=sm["den"][:]: e.reciprocal(out=o, in_=o), reads=[k("den")], writes=[k("den")])
    v(ALU.mult, "t1", "nr", "lr"); v(ALU.mult, "t2", "ai", "li"); v(ALU.add, "t1", "t1", "t2"); v(ALU.mult, "cor", "t1", "den")
    v(ALU.mult, "t3", "ai", "lr"); v(ALU.mult, "t4", "nr", "li"); v(ALU.subtract, "t3", "t3", "t4"); v(ALU.mult, "coi", "t3", "den")
    bsrc = [C.d["s5_b_re"][0], C.d["s5_b_im"][0]]
    for i in range(2):
        for half in range(2):
            dma(P, Bn[i][:, half * 16:(half + 1) * 16, :],
                bsrc[i].rearrange("(gp g2) p h -> (g2 p) gp h", g2=2)[:, half * 16:(half + 1) * 16, :], reads=[], writes=[("Bn", i)])
    cor16 = v16(6)
    coi16 = v16(7)
    for h in range(16):
        cp(P, "dve", cor16[:, :, h], sm["cor"][:], reads=[k("cor")], writes=["cor16"])
        cp(P, "pool", coi16[:, :, h], sm["coi"][:], reads=[k("coi")], writes=["coi16"])
    tt(P, "dve", bt[0][:], cor16[:], Bn[0][:], ALU.mult, reads=["cor16", ("Bn", 0)], writes=["bt0"])
    tt(P, "dve", bt[1][:], coi16[:], Bn[1][:], ALU.mult, reads=["coi16", ("Bn", 1)], writes=["bt1"])
    tt(P, "dve", Bb[0][:], bt[0][:], bt[1][:], ALU.subtract, reads=["bt0", "bt1"], writes=[("Bb", 0)])
    tt(P, "dve", bt[0][:], cor16[:], Bn[1][:], ALU.mult, reads=["cor16", ("Bn", 1), ("Bb", 0)], writes=["bt0"])
    tt(P, "dve", bt[1][:], coi16[:], Bn[0][:], ALU.mult, reads=["coi16", ("Bn", 0), ("Bb", 0)], writes=["bt1"])
    tt(P, "dve", Bb[1][:], bt[0][:], bt[1][:], ALU.add, reads=["bt0", "bt1"], writes=[("Bb", 1)])
    for i in range(2):
        for gp in range(32):
            c0 = ((2 * gp) % 8) * 16
            memset(P, "pool", Z[:], 0.0, writes=["Zs5"])
            cp(P, "pool", Z[0:64, c0:c0 + 16], Bb[i][0:64, gp, :], reads=[("Bb", i)], writes=["Zs5"])
            cp(P, "pool", Z[64:128, c0 + 16:c0 + 32], Bb[i][64:128, gp, :], reads=[("Bb", i)], writes=["Zs5"])
            pb = C.ps[gp % 2]
            tp(P, pb[:, 0:128], Z[:], C.ident[:], reads=["Zs5", "ident"], writes=["ps%d" % (gp % 2)])
            cp(P, "act", Bblk[i][:, gp, :], pb[:, 0:128], reads=["ps%d" % (gp % 2)], writes=[("Bblk", i)])
    csrc = [C.d["s5_c_re"][0], C.d["s5_c_im"][0]]
    for i in range(2):
        memset(P, "pool", Cblk[i][:], 0.0, writes=[("Cblk", i)])
        for half in range(2):
            dma(P, Cnat[i][:], csrc[i].rearrange("g h p -> h g p")[:, half * 32:(half + 1) * 32, :], reads=[], writes=["Cnat"])
            for gp in range(half * 16, (half + 1) * 16):
                c0 = ((2 * gp) % 8) * 16
                gl = 2 * gp - half * 32
                pb = C.ps[2 + gp % 2]
                pk = "ps%d" % (2 + gp % 2)
                tp(P, pb[:, 0:16], Cnat[i][:, gl:gl + 2, :].rearrange("h a p -> h (a p)"), C.ident[0:16, 0:16],
                   reads=["Cnat", "ident"], writes=[pk])
                sc = 1.0 if i == 0 else -1.0
                act(P, Cblk[i][0:64, gp, c0:c0 + 16], pb[0:64, 0:16], AF.Copy, reads=[pk], writes=[("Cblk", i)], scale=sc)
                act(P, Cblk[i][64:128, gp, c0 + 16:c0 + 32], pb[64:128, 0:16], AF.Copy, reads=[pk], writes=[("Cblk", i)], scale=sc)
    fence = sb("s5fence", [128, 1], F32)
    memset(P, "dve", fence[:], 0.0,
           writes=["s5fence", "s5T1", "s5T2", "s5T3", "Cnat", ("Bn", 0), ("Bn", 1), ("Bb", 0), ("Bb", 1), "bt0", "bt1", "cor16", "coi16"])
    load_cols(C, tmp, dcol[:], C.d["s5_d"][0].rearrange("(a b) h -> a (b h)", b=8), 8, "dcol")
    for dt in range(DT):
        dma(P, wglu[:, dt, :], C.d["s5_w_glu"][0][dt * 128:(dt + 1) * 128, :], reads=[], writes=[("wglu", dt)], eng="pool")
        dma(P, wout[:, dt, :], C.d["s5_w_out"][0][dt * 128:(dt + 1) * 128, :], reads=[], writes=[("wout", dt)], eng="pool")

    dump = DBG.get("s5dump")
    if dump:
        dbg = C.d["dbg"]
        dma(P, dbg[0, :, 0:32], sm["ar"][:], reads=[k("ar")], writes=["dbg0"])
        dma(P, dbg[0, :, 32:64], sm["ai"][:], reads=[k("ai")], writes=["dbg0"])
        dma(P, dbg[0, :, 64:96], sm["cor"][:], reads=[k("cor")], writes=["dbg0"])
        dma(P, dbg[0, :, 96:128], sm["coi"][:], reads=[k("coi")], writes=["dbg0"])
        dma(P, dbg[0, :, 128:160], sm["dt"][:], reads=[k("dt")], writes=["dbg0"])
        dma(P, dbg[1, :, 0:512], Bb[0][:].rearrange("p a b -> p (a b)"), reads=[("Bb", 0)], writes=["dbg1"])
        dma(P, dbg[1, :, 512:1024], Bb[1][:].rearrange("p a b -> p (a b)"), reads=[("Bb", 1)], writes=["dbg1"])
        dma(P, dbg[5, :, 0:4096], Bblk[0][:].rearrange("p a b -> p (a b)"), reads=[("Bblk", 0)], writes=["dbg5"])
        dma(P, dbg[6, :, 0:4096], Cblk[0][:].rearrange("p a b -> p (a b)"), reads=[("Cblk", 0)], writes=["dbg6"])
    nch = DBG.get("nch", NCH)
    buk = [("BU", 0), ("BU", 1)]
    bu = BU[0]
    fl = lambda t_: t_[:].rearrange("p g t -> p (g t)")

    def load_x(c):
        for s_ in range(2):
            tok = s_ * SEQ + c * TC
            dma(P, xc[c % 2][:, :, s_ * TC:(s_ + 1) * TC], xT[:, :, tok:tok + TC].rearrange("d p t -> p d t"), reads=[],
                writes=["xc%d" % (c % 2)])
            dma(P, xcb[c % 2][:, :, s_ * TC:(s_ + 1) * TC], xT[:, :, tok:tok + TC].rearrange("d p t -> p d t"), reads=[],
                writes=["xcb%d" % (c % 2)], eng="pool")

    def bu_stage(c):
        xb, xbk = xcb[c % 2], "xcb%d" % (c % 2)
        for g4 in range(8):
            for i in range(2):
                pb = C.ps[i]
                for j in range(4):
                    gp = g4 * 4 + j
                    mm(P, pb[:, j * W:(j + 1) * W], Bblk[i][:, gp, :], xb[:, gp // 4, :], True, True,
                       reads=[("Bblk", i), xbk], writes=["ps%d" % i], inc=(j == 3))
                cp(P, "act", bu[i][:, g4 * 4:(g4 + 1) * 4, :].rearrange("p a b -> p (a b)"), pb[:, 0:4 * W], reads=["ps%d" % i], writes=[buk[i]])

    def scan_gen(c):
        for s_ in range(2):
            X = bu[0][:, :, s_ * TC:(s_ + 1) * TC]
            Y = bu[1][:, :, s_ * TC:(s_ + 1) * TC]
            hp_r, hp_i = hprev[s_]
            kr, ki = ("hp", s_, 0), ("hp", s_, 1)
            tt(P, "dve", T1[:], cosT[:], X, ALU.mult, reads=["s5cos", buk[0]], writes=["s5T1"])
            tt(P, "dve", T2[:], sinT[:], Y, ALU.mult, reads=["s5sin", buk[1]], writes=["s5T2"])
            tt(P, "dve", T1[:], T1[:], T2[:], ALU.add, reads=["s5T1", "s5T2"], writes=["s5T1"])
            yield
            tt(P, "dve", T2[:], cosT[:], Y, ALU.mult, reads=["s5cos", buk[1]], writes=["s5T2"])
            tt(P, "dve", T3[:], sinT[:], X, ALU.mult, reads=["s5sin", buk[0]], writes=["s5T3"])
            tt(P, "dve", T2[:], T2[:], T3[:], ALU.subtract, reads=["s5T2", "s5T3"], writes=["s5T2"])
            yield
            tt(P, "dve", sm["t1"][:], sm["er"][:], hp_r[:], ALU.mult, reads=[k("er"), kr], writes=[k("t1")])
            tt(P, "dve", T1[:, :, 0], T1[:, :, 0], sm["t1"][:], ALU.add, reads=["s5T1", k("t1")], writes=["s5T1"])
            tt(P, "dve", sm["t2"][:], sm["er"][:], hp_i[:], ALU.mult, reads=[k("er"), ki], writes=[k("t2")])
            tt(P, "dve", T2[:, :, 0], T2[:, :, 0], sm["t2"][:], ALU.add, reads=["s5T2", k("t2")], writes=["s5T2"])
            P.op("dve", lambda e, o=fl(T3), d0=fl(rT), d1=fl(T1): e.tensor_tensor_scan(out=o, data0=d0, data1=d1, initial=0.0,
                                                                                      op0=ALU.mult, op1=ALU.add),
                 reads=["s5rT", "s5T1"], writes=["s5T3"])
            yield
            P.op("dve", lambda e, o=fl(T1), d0=fl(rT), d1=fl(T2): e.tensor_tensor_scan(out=o, data0=d0, data1=d1, initial=0.0,
                                                                                      op0=ALU.mult, op1=ALU.add),
                 reads=["s5rT", "s5T2"], writes=["s5T1"])
            yield
            tt(P, "dve", X, cosT[:], T3[:], ALU.mult, reads=["s5cos", "s5T3"], writes=[buk[0]])
            tt(P, "dve", T2[:], sinT[:], T1[:], ALU.mult, reads=["s5sin", "s5T1"], writes=["s5T2"])
            tt(P, "dve", X, X, T2[:], ALU.subtract, reads=[buk[0], "s5T2"], writes=[buk[0]])
            yield
            tt(P, "dve", Y, sinT[:], T3[:], ALU.mult, reads=["s5sin", "s5T3"], writes=[buk[1]])
            tt(P, "dve", T2[:], cosT[:], T1[:], ALU.mult, reads=["s5cos", "s5T1"], writes=["s5T2"])
            tt(P, "dve", Y, Y, T2[:], ALU.add, reads=[buk[1], "s5T2"], writes=[buk[1]])
            cp(P, "dve", hp_r[:], X[:, :, TC - 1], reads=[buk[0]], writes=[kr])
            cp(P, "dve", hp_i[:], Y[:, :, TC - 1], reads=[buk[1]], writes=[ki])
            yield

    def cast_stage(c):
        for i in range(2):
            cp(P, "act", hb16[i][:], bu[i][:], reads=[buk[i]], writes=[("hb16", i)])

    def tail_gen(c):
        x, xk = xc[c % 2], "xc%d" % (c % 2)
        for ct in range(DT):
            pb = C.ps[2 + ct % 2]
            pk = "ps%d" % (2 + ct % 2)
            n = 0
            for gp in range(4 * ct, 4 * ct + 4):
                for i in range(2):
                    mm(P, pb[:, 0:W], Cblk[i][:, gp, :], hb16[i][:, gp, :], n == 0, n == 7, reads=[("Cblk", i), ("hb16", i)], writes=[pk])
                    n += 1
            stt(P, ys[:, ct, :], x[:, ct, :], dcol[:, ct:ct + 1], pb[:, 0:W], ALU.mult, ALU.add, reads=[xk, "dcol", pk], writes=[("ys", ct)])
            act(P, ygf[:, ct, :], ys[:, ct, :], AF.Gelu_apprx_tanh, reads=[("ys", ct)], writes=[("ygf", ct)])
            cp(P, "pool", ygb[:, ct, :], ygf[:, ct, :], reads=[("ygf", ct)], writes=[("ygb", ct)])
            yield
        for ct in range(DT):
            bank = 4 if ct % 2 == 0 else 7
            pb = C.ps[bank]
            pk = "ps%d" % bank
            for kt in range(DT):
                mm(P, pb[:, 0:W], wglu[:, kt, ct * 128:(ct + 1) * 128], ygb[:, kt, :], kt == 0, kt == DT - 1,
                   reads=[("wglu", kt), ("ygb", kt)], writes=[pk])
            sg = sig[ct % 2]
            sgk = "sig%d" % (ct % 2)
            act(P, sg[:], pb[:, 0:W], AF.Sigmoid, reads=[pk, "bglu"], writes=[sgk], bias=C.bglu[:, ct:ct + 1])
            tt(P, "dve", gb[:, ct, :], ygf[:, ct, :], sg[:], ALU.mult, reads=[("ygf", ct), sgk], writes=[("gb", ct)])
            yield
        for dt in range(DT):
            pb = C.ps[2 + dt % 2]
            pk = "ps%d" % (2 + dt % 2)
            for kt in range(DT):
                mm(P, pb[:, 0:W], wout[:, kt, dt * 128:(dt + 1) * 128], gb[:, kt, :], kt == 0, kt == DT - 1,
                   reads=[("wout", kt), ("gb", kt)], writes=[pk])
            stt(P, ys[:, dt, :], x[:, dt, :], ALPHA, pb[:, 0:W], ALU.mult, ALU.add, reads=[xk, pk], writes=[("ys", dt)])
            yield
        houtk = [("s5hout", dt) for dt in range(DT)]
        for _ in ln_gen(C, lt, ys, [("ys", dt) for dt in range(DT)], hout, houtk, 0, NB=W):
            yield
        for s_ in range(2):
            tok = s_ * SEQ + c * TC
            b = tok // NB
            dma(P, haT[:, :, tok:tok + TC].rearrange("d p t -> p d t"), hout[:, :, s_ * TC:(s_ + 1) * TC], reads=houtk, writes=[("haT", b)])

    if nch > 0:
        load_x(0)
        bu_stage(0)
        for _ in scan_gen(0):
            pass
        cast_stage(0)
    for c in range(nch):
        if c + 1 < nch:
            load_x(c + 1)
            bu_stage(c + 1)
            sg, tg = scan_gen(c + 1), tail_gen(c)
            done_s = done_t = False
            while not (done_s and done_t):
                if not done_s:
                    done_s = next(sg, "end") == "end"
                for _ in range(4):
                    if not done_t:
                        done_t = next(tg, "end") == "end"
            cast_stage(c + 1)
        else:
            for _ in tail_gen(c):
                pass
```

```python
import math
from contextlib import ExitStack
import numpy as np
import concourse.bass as bass
import concourse.mybir as mybir
from concourse.bass_utils import run_bass_kernel_spmd

F32 = mybir.dt.float32
BF16 = mybir.dt.bfloat16
AF = mybir.ActivationFunctionType
ALU = mybir.AluOpType

ENGS = ("pe", "act", "dve", "pool", "sp")
N_DMA_SEMS = 16
SAME_ENG_SYNC = True
SAME_ENG_MIN_DIST = 2

D = 1024
DT = 8
SEQ = 2048
NSEQ = 2
NTOK = NSEQ * SEQ
NB = 256
NBLK = NTOK // NB
BPS = SEQ // NB
DFF = 2816
FT = 44
FH = 22
ALPHA = (2.0 * 2) ** 0.25
LN_EPS = 1e-5


class Prog:
    def __init__(self, nc):
        self.nc = nc
        self.q = {e: [] for e in ENGS}
        self.cnt = {e: 0 for e in ENGS}
        self.seen = {e: {} for e in ENGS}
        self.state = {}
        self.dma_rr = {e: 0 for e in ENGS}
        self.dma_cnt = {}
        self.n_ops = 0

    def _deps(self, reads, writes):
        deps = set()
        for k in reads:
            st = self.state.get(k)
            if st is not None and st[0] is not None:
                deps.add(st[0])
        for k in writes:
            st = self.state.get(k)
            if st is not None:
                if st[0] is not None:
                    deps.add(st[0])
                deps.update(st[1])
        return deps

    def _waits(self, eng, deps):
        best = {}
        for (sk, v) in deps:
            if best.get(sk, 0) < v:
                best[sk] = v
        waits = []
        for sk, v in best.items():
            if sk == eng and (eng == "pe" or not SAME_ENG_SYNC):
                continue
            if sk == eng and self.cnt[eng] - v >= SAME_ENG_MIN_DIST - 1:
                continue
            if self.seen[eng].get(sk, 0) < v:
                self.seen[eng][sk] = v
                waits.append((sk, v))
        return waits

    def _commit(self, ticket, reads, writes):
        for k in reads:
            st = self.state.get(k)
            if st is None:
                self.state[k] = [None, [ticket]]
            else:
                st[1].append(ticket)
                if len(st[1]) > 48:
                    best = {}
                    for (sk, v) in st[1]:
                        if best.get(sk, 0) < v:
                            best[sk] = v
                    st[1] = list(best.items())
        for k in writes:
            self.state[k] = [ticket, []]

    def op(self, eng, fn, reads=(), writes=(), inc=True):
        deps = self._deps(reads, writes)
        waits = self._waits(eng, deps)
        ticket = (eng, self.cnt[eng] + 1)
        if inc:
            self.cnt[eng] += 1
        self.q[eng].append((waits, fn, (eng, 1) if inc else None))
        self._commit(ticket, reads, writes)
        self.n_ops += 1
        return ticket

    def dma(self, eng, fn, reads=(), writes=()):
        deps = self._deps(reads, writes)
        waits = self._waits(eng, deps)
        i = self.dma_rr[eng]
        self.dma_rr[eng] = (i + 1) % N_DMA_SEMS
        sk = ("dma", eng, i)
        v = self.dma_cnt.get(sk, 0) + 16
        self.dma_cnt[sk] = v
        ticket = (sk, v)
        if v > 16 and self.seen[eng].get(sk, 0) < v - 16:
            self.seen[eng][sk] = v - 16
            waits = waits + [(sk, v - 16)]
        self.q[eng].append((waits, fn, (sk, 16)))
        self._commit(ticket, reads, writes)
        self.n_ops += 1
        return ticket

    def wait_all(self, eng, keys):
        deps = set()
        for k in keys:
            st = self.state.get(k)
            if st is not None:
                if st[0] is not None:
                    deps.add(st[0])
                deps.update(st[1])
        waits = self._waits(eng, deps)
        self.q[eng].append((waits, None, None))

    @staticmethod
    def all_sem_keys():
        ks = list(ENGS)
        for e in ("sp", "pool", "act"):
            for i in range(N_DMA_SEMS):
                ks.append(("dma", e, i))
        return ks

    def check(self):
        if not hasattr(self, "simv"):
            self.simv = {}
        pos = {e: 0 for e in ENGS}
        while True:
            progress = False
            for e in ENGS:
                q = self.q[e]
                while pos[e] < len(q):
                    waits, fn, inc = q[pos[e]]
                    if all(self.simv.get(sk, 0) >= v for sk, v in waits):
                        if inc is not None:
                            self.simv[inc[0]] = self.simv.get(inc[0], 0) + inc[1]
                        pos[e] += 1
                        progress = True
                    else:
                        break
            if all(pos[e] == len(self.q[e]) for e in ENGS):
                return
            if not progress:
                msg = []
                for e in ENGS:
                    if pos[e] < len(self.q[e]):
                        waits = self.q[e][pos[e]][0]
                        msg.append("%s@%d waits %s (have %s)" % (e, pos[e], waits, [self.simv.get(sk, 0) for sk, _ in waits]))
                raise RuntimeError("DEADLOCK: " + "; ".join(msg))

    def emit(self, block, sems):
        self.check()
        qs = self.q
        self.q = {e: [] for e in ENGS}

        def run(engname):
            def body(h):
                for waits, fn, inc in qs[engname]:
                    for (sk, v) in waits:
                        h.wait_ge(sems[sk], v)
                    if fn is not None:
                        ins = fn(h)
                        if inc is not None:
                            ins.then_inc(sems[inc[0]], inc[1])
            return body
        block.tensor(run("pe"))
        block.scalar(run("act"))
        block.vector(run("dve"))
        block.gpsimd(run("pool"))
        block.sync(run("sp"))


def mm(P, out, lhsT, rhs, start, stop, reads, writes, inc=None):
    if inc is None:
        inc = stop
    return P.op("pe", lambda e: e.matmul(out, lhsT=lhsT, rhs=rhs, start=start, stop=stop),
                reads=reads, writes=writes, inc=inc)


def tp(P, out, in_, ident, reads, writes, inc=True):
    return P.op("pe", lambda e: e.transpose(out=out, in_=in_, identity=ident), reads=reads, writes=writes, inc=inc)


def act(P, out, in_, func, reads, writes, scale=None, bias=None):
    kw = {}
    if scale is not None:
        kw["scale"] = scale
    if bias is not None:
        kw["bias"] = bias
    return P.op("act", lambda e: e.activation(out=out, in_=in_, func=func, **kw), reads=reads, writes=writes)


def tt(P, eng, out, in0, in1, op, reads, writes):
    return P.op(eng, lambda e: e.tensor_tensor(out=out, in0=in0, in1=in1, op=op), reads=reads, writes=writes)


def stt(P, out, in0, scalar, in1, op0, op1, reads, writes):
    return P.op("dve", lambda e: e.scalar_tensor_tensor(out=out, in0=in0, scalar=scalar, in1=in1, op0=op0, op1=op1),
                reads=reads, writes=writes)


def ts(P, eng, out, in0, s1, s2, op0, op1, reads, writes):
    return P.op(eng, lambda e: e.tensor_scalar(out=out, in0=in0, scalar1=s1, scalar2=s2, op0=op0, op1=op1),
                reads=reads, writes=writes)


def cp(P, eng, out, in_, reads, writes):
    if eng == "act":
        return act(P, out, in_, AF.Copy, reads, writes)
    return P.op(eng, lambda e: e.tensor_copy(out=out, in_=in_), reads=reads, writes=writes)


def memset(P, eng, ap, val, writes):
    return P.op(eng, lambda e: e.memset(ap, val), writes=writes)


def dma(P, out, in_, reads, writes, eng="sp"):
    return P.dma(eng, lambda e: e.dma_start(out=out, in_=in_), reads=reads, writes=writes)


class Ctx:
    pass


DBG = {}


def run_phase(C, fn):
    C.phase_idx = getattr(C, "phase_idx", 0) + 1
    pfx = "p%d_" % C.phase_idx
    with ExitStack() as es:
        def sb(name, shape, dt):
            return es.enter_context(C.nc.sbuf_tensor(pfx + name, shape, dt))
        fn(C, sb)
        with C.nc.Block() as block:
            C.P.emit(block, C.sems)


def load_cols(C, sb_tmp, dst, src2d, n, key_dst):
    P = C.P
    dma(P, sb_tmp[0:n, :], src2d, reads=[], writes=["lc_tmp"])
    tp(P, C.ps[7][:, 0:n], sb_tmp[0:n, :], C.ident[0:n, 0:n], reads=["lc_tmp", "ident"], writes=["ps7"])
    cp(P, "dve", dst, C.ps[7][:, 0:n], reads=["ps7"], writes=[key_dst])


def phase_setup(C, sb):
    P = C.P
    tmp = sb("lc_tmp", [128, 128], F32)
    dma(P, C.ident[:], C.d["ident"], reads=[], writes=["ident"])
    memset(P, "pool", C.onesb[:], 1.0 / 1024.0, writes=["onesb"])
    for l in range(2):
        for k in range(3):
            load_cols(C, tmp, C.cw[:, l, k, :], C.d["ffn_conv_w"][l, k].rearrange("(a b) -> a b", b=128), FT, "cw")
        load_cols(C, tmp, C.cb[:, l, :], C.d["ffn_conv_b"][l].rearrange("(a b) -> a b", b=128), FT, "cb")
    load_cols(C, tmp, C.lng[:], C.d["ln_gain"].rearrange("l i (a b) -> (l i a) b", b=128), 32, "lng")
    load_cols(C, tmp, C.lnb[:], C.d["ln_bias"].rearrange("l i (a b) -> (l i a) b", b=128), 32, "lnb")
    load_cols(C, tmp, C.bglu[:], C.d["s5_b_glu"][0].rearrange("(a b) -> a b", b=128), 8, "bglu")


def ln_gen(C, sb_t, z, zkeys, hout, hkeys, ln_idx, NB=NB):
    P = C.P
    mean_ps, ex2_ps = C.ps[5], C.ps[6]
    for dt in range(DT):
        zb = sb_t["zb"][dt % 2]
        zq = sb_t["zq"][dt % 2]
        kb, kq = "zb%d" % (dt % 2), "zq%d" % (dt % 2)
        cp(P, "act", zb[:], z[:, dt, :], reads=[zkeys[dt]], writes=[kb])
        tt(P, "pool", zq[:], z[:, dt, :], z[:, dt, :], ALU.mult, reads=[zkeys[dt]], writes=[kq])
        mm(P, mean_ps[:, 0:NB], C.onesb[:], zb[:], dt == 0, dt == DT - 1, reads=[kb, "onesb"], writes=["ps5"], inc=True)
        mm(P, ex2_ps[:, 0:NB], C.onesb[:], zq[:], dt == 0, dt == DT - 1, reads=[kq, "onesb"], writes=["ps6"], inc=True)
        yield
    mean = sb_t["mean"]
    m2 = sb_t["m2"]
    rstd = sb_t["rstd"]
    nmr = sb_t["nmr"]
    cp(P, "act", mean[:], mean_ps[:, 0:NB], reads=["ps5"], writes=["mean"])
    tt(P, "pool", m2[:], mean[:], mean[:], ALU.mult, reads=["mean"], writes=["m2"])
    tt(P, "dve", m2[:], ex2_ps[:, 0:NB], m2[:], ALU.subtract, reads=["ps6", "m2"], writes=["m2"])
    act(P, rstd[:], m2[:], AF.Sqrt, reads=["m2", "epsc"], writes=["rstd"], bias=C.epsc[:, 0:1])
    P.op("dve", lambda e: e.reciprocal(out=rstd[:], in_=rstd[:]), reads=["rstd"], writes=["rstd"])
    stt(P, nmr[:], mean[:], -1.0, rstd[:], ALU.mult, ALU.mult, reads=["mean", "rstd"], writes=["nmr"])
    yield
    for dt in range(DT):
        y = sb_t["y"][dt % 2]
        ky = "y%d" % (dt % 2)
        tt(P, "dve", y[:], z[:, dt, :], rstd[:], ALU.mult, reads=[zkeys[dt], "rstd"], writes=[ky])
        tt(P, "pool", y[:], y[:], nmr[:], ALU.add, reads=[ky, "nmr"], writes=[ky])
        act(P, hout[:, dt, :], y[:], AF.Identity, reads=[ky, "lng", "lnb"], writes=[hkeys[dt]],
            scale=C.lng[:, ln_idx * 8 + dt: ln_idx * 8 + dt + 1], bias=C.lnb[:, ln_idx * 8 + dt: ln_idx * 8 + dt + 1])
        yield


def layer_norm_block(C, sb_t, z, zkeys, hout, hkeys, ln_idx, NB=NB):
    for _ in ln_gen(C, sb_t, z, zkeys, hout, hkeys, ln_idx, NB=NB):
        pass


def ln_tiles(sb, NB=NB):
    t = {}
    t["zb"] = [sb("zb%d" % i, [128, NB], BF16) for i in range(2)]
    t["zq"] = [sb("zq%d" % i, [128, NB], BF16) for i in range(2)]
    t["y"] = [sb("y%d" % i, [128, NB], F32) for i in range(2)]
    for n in ("mean", "m2", "rstd", "nmr"):
        t[n] = sb(n, [128, NB], F32)
    return t


def make_phase_ffn(layer, src, dst):
    def phase(C, sb):
        P = C.P
        inT, outT = C.d[src], C.d[dst]
        wup = sb("wup", [128, DT, 2 * DFF], BF16)
        wdn = sb("wdn", [128, FH, D], BF16)
        hin = [sb("hin%d" % i, [128, DT, NB], F32) for i in range(2)]
        hbs = [sb("hb%d" % i, [128, DT, NB], BF16) for i in range(2)]
        NHC = 4
        hc = [sb("hc%d" % i, [128, NB + 2], F32) for i in range(NHC)]
        tb = [sb("tb%d" % i, [128, NB], F32) for i in range(4)]
        sl = [sb("sl%d" % i, [128, NB], F32) for i in range(2)]
        actts = [sb("actt%d" % i, [128, FH, NB], BF16) for i in range(2)]
        halo = sb("halo", [128, FT, 2], F32)
        hout = sb("hout", [128, DT, NB], F32)
        lt = ln_tiles(sb)
        wu, wd = C.d["ffn_w_up"][layer], C.d["ffn_w_down"][layer]
        for dt in range(DT):
            for c0 in range(0, 2 * DFF, 1024):
                c1 = min(c0 + 1024, 2 * DFF)
                dma(P, wup[:, dt, c0:c1], wu[dt * 128:(dt + 1) * 128, c0:c1], reads=[], writes=[("wup", dt, c0)], eng="pool")
        for ft in range(FH):
            dma(P, wdn[:, ft, :], wd[ft * 128:(ft + 1) * 128, :], reads=[], writes=[("wdn", ft)], eng="pool")

        def wup_keys(dt, col):
            return ("wup", dt, (col // 1024) * 1024)

        UPB = (0, 1, 2, 7)

        def load_blk(b):
            h = hin[b % 2]
            dma(P, h[:], inT[:, :, b * NB:(b + 1) * NB].rearrange("d p t -> p d t"), reads=[(src, b)],
                writes=[("hin", b % 2, dt) for dt in range(DT)])
            for dt in range(DT):
                cp(P, "pool", hbs[b % 2][:, dt, :], h[:, dt, :], reads=[("hin", b % 2, dt)], writes=[("hb", b % 2, dt)])

        def up_pair(b, i):
            hb = hbs[b % 2]
            if i == 0 and b % BPS == 0:
                memset(P, "pool", halo[:], 0.0, writes=[("halo", ft) for ft in range(FT)])
            for which in range(2):
                ft = i + which * FH
                j = i * 2 + which
                pb = C.ps[UPB[j % 4]]
                pk = "ps%d" % UPB[j % 4]
                for dt in range(DT):
                    mm(P, pb[:, 0:NB], wup[:, dt, ft * 128:(ft + 1) * 128], hb[:, dt, :], dt == 0, dt == DT - 1,
                       reads=[wup_keys(dt, ft * 128), ("hb", b % 2, dt)], writes=[pk])
                hcb = hc[j % NHC]
                hck = "hc%d" % (j % NHC)
                hhk = "hch%d" % (j % NHC)
                cp(P, "pool", hcb[:, 0:2], halo[:, ft, :], reads=[("halo", ft)], writes=[hhk])
                cp(P, "act", hcb[:, 2:NB + 2], pb[:, 0:NB], reads=[pk], writes=[hck])
                cp(P, "pool", halo[:, ft, :], hcb[:, NB:NB + 2], reads=[hck], writes=[("halo", ft)])
                t = tb[j % 4]
                tk = "tb%d" % (j % 4)
                act(P, t[:], hcb[:, 2:NB + 2], AF.Identity, reads=[hck, "cw", "cb"], writes=[tk],
                    scale=C.cw[:, layer, 0, ft:ft + 1], bias=C.cb[:, layer, ft:ft + 1])
                stt(P, t[:], hcb[:, 1:NB + 1], C.cw[:, layer, 1, ft:ft + 1], t[:], ALU.mult, ALU.add,
                    reads=[hck, hhk, tk, "cw"], writes=[tk])
                stt(P, t[:], hcb[:, 0:NB], C.cw[:, layer, 2, ft:ft + 1], t[:], ALU.mult, ALU.add,
                    reads=[hck, hhk, tk, "cw"], writes=[tk])
                if which == 1:
                    s = sl[i % 2]
                    sk = "sl%d" % (i % 2)
                    act(P, s[:], t[:], AF.Silu, reads=[tk], writes=[sk])
                    tv = tb[(j - 1) % 4]
                    tvk = "tb%d" % ((j - 1) % 4)
                    tt(P, "dve", actts[b % 2][:, i, :], s[:], tv[:], ALU.mult, reads=[sk, tvk], writes=[("actt", b % 2, i)])

        def down_dt(b, dt):
            h = hin[b % 2]
            hkd = ("hin", b % 2, dt)
            pb = C.ps[3 + dt % 2]
            pk = "ps%d" % (3 + dt % 2)
            for i in range(FH):
                mm(P, pb[:, 0:NB], wdn[:, i, dt * 128:(dt + 1) * 128], actts[b % 2][:, i, :], i == 0, i == FH - 1,
                   reads=[("wdn", i), ("actt", b % 2, i)], writes=[pk])
            stt(P, h[:, dt, :], h[:, dt, :], ALPHA, pb[:, 0:NB], ALU.mult, ALU.add, reads=[hkd, pk], writes=[hkd])

        KA, KB = 8, 9
        nblk = DBG.get("nblk", NBLK)
        load_blk(0)
        for i in range(FH):
            up_pair(0, i)
        for b in range(nblk):
            h = hin[b % 2]
            hk = [("hin", b % 2, dt) for dt in range(DT)]
            nxt = b + 1 < nblk
            if nxt:
                load_blk(b + 1)
            for dt in range(DT):
                down_dt(b, dt)
                if nxt and dt < KA:
                    up_pair(b + 1, dt)
            houtk = [("hout", dt) for dt in range(DT)]
            g = ln_gen(C, lt, h, hk, hout, houtk, layer * 2 + 1)
            if nxt:
                for i in range(KA, KA + KB):
                    up_pair(b + 1, i)
                    next(g, None)
                    next(g, None)
            for _ in g:
                pass
            dma(P, outT[:, :, b * NB:(b + 1) * NB].rearrange("d p t -> p d t"), hout[:], reads=houtk, writes=[(dst, b)])
            if nxt:
                for i in range(KA + KB, FH):
                    up_pair(b + 1, i)
    return phase


WEIGHT_NAMES = ["s5_lam_re", "s5_lam_im", "s5_log_dt", "s5_b_re", "s5_b_im", "s5_c_re", "s5_c_im", "s5_d",
                "s5_w_glu", "s5_b_glu", "s5_w_out", "attn_w_kv", "attn_w_q", "attn_w_out", "rel_bias",
                "ffn_w_up", "ffn_conv_w", "ffn_conv_b", "ffn_w_down", "ln_gain", "ln_bias"]
WEIGHT_SHAPES = {
    "s5_lam_re": (1, 64, 64), "s5_lam_im": (1, 64, 64), "s5_log_dt": (1, 64), "s5_b_re": (1, 64, 64, 16),
    "s5_b_im": (1, 64, 64, 16), "s5_c_re": (1, 64, 16, 64), "s5_c_im": (1, 64, 16, 64), "s5_d": (1, 64, 16),
    "s5_w_glu": (1, 1024, 1024), "s5_b_glu": (1, 1024), "s5_w_out": (1, 1024, 1024), "attn_w_kv": (1024, 6144),
    "attn_w_q": (1, 1024, 3072), "attn_w_out": (1, 1024, 1024), "rel_bias": (32, 48), "ffn_w_up": (2, 1024, 5632),
    "ffn_conv_w": (2, 3, 5632), "ffn_conv_b": (2, 5632), "ffn_w_down": (2, 2816, 1024), "ln_gain": (2, 2, 1024),
    "ln_bias": (2, 2, 1024)}


def build_program(phases, ext_in=(), ext_out=()):
    nc = bass.Bass("TRN2", target_bir_lowering=False)
    C = Ctx()
    C.nc = nc
    C.P = Prog(nc)
    C.d = {}
    for n in WEIGHT_NAMES:
        C.d[n] = nc.dram_tensor(n, list(WEIGHT_SHAPES[n]), F32, kind="ExternalInput").ap()
    C.d["ident"] = nc.dram_tensor("ident", [128, 128], F32, kind="ExternalInput").ap()
    C.d["antiI"] = nc.dram_tensor("antiI", [128, 128], F32, kind="ExternalInput").ap()
    C.d["onehot"] = nc.dram_tensor("onehot", [33, 3, 384], F32, kind="ExternalInput").ap()
    C.d_handles = {}
    C.d_handles["ebf"] = nc.dram_tensor("ebf", [48, 384], F32, kind="Internal")
    C.d["ebf"] = C.d_handles["ebf"].ap()
    if DBG.get("s5dump"):
        C.d["dbg"] = nc.dram_tensor("dbg", [8, 128, 4096], F32, kind="ExternalOutput").ap()
    acts = ["xT", "haT", "h1T", "hcT", "outT"]
    for n in acts:
        kind = "Internal"
        if n in ext_in or n == "xT":
            kind = "ExternalInput"
        if n in ext_out or n == "outT":
            kind = "ExternalOutput"
        C.d[n] = nc.dram_tensor(n, [DT, 128, NTOK], F32, kind=kind).ap()
    with ExitStack() as es:
        def sbg(name, shape, dt):
            return es.enter_context(nc.sbuf_tensor(name, shape, dt))
        C.ident = sbg("ident_sb", [128, 128], F32)
        C.onesb = sbg("onesb", [128, 128], BF16)
        C.cw = sbg("cw", [128, 2, 3, FT], F32)
        C.cb = sbg("cb", [128, 2, FT], F32)
        C.lng = sbg("lng", [128, 32], F32)
        C.lnb = sbg("lnb", [128, 32], F32)
        C.bglu = sbg("bglu", [128, 8], F32)
        C.epsc = sbg("epsc", [128, 1], F32)
        C.ps = [es.enter_context(nc.psum_tensor("psb%d" % i, [128, 512], F32)) for i in range(8)]
        C.sems = {}
        for k in Prog.all_sem_keys():
            nm = "s_" + "_".join(str(x) for x in (k if isinstance(k, tuple) else (k,)))
            C.sems[k] = es.enter_context(nc.semaphore(nm))
        memset(C.P, "dve", C.epsc[:], LN_EPS, writes=["epsc"])
        table = {
            "setup": phase_setup,
            "ffn0": make_phase_ffn(0, "haT", "h1T"),
            "ffn1": make_phase_ffn(1, "hcT", "outT"),
            "attn": phase_attn,
            "s5": phase_s5,
        }
        for ph in phases:
            run_phase(C, table[ph])
        def fin(C, sb):
            C.P.wait_all("sp", [(n, b) for n in ("outT", "haT", "h1T", "hcT") for b in range(NBLK)] + ["dbg%d" % i for i in range(8)])
        run_phase(C, fin)
    return nc


def host_consts():
    return {"ident": np.eye(128, dtype=np.float32), "onehot": host_onehot(),
            "antiI": np.ascontiguousarray(np.eye(128, dtype=np.float32)[::-1])}


def kernel(**inputs):
    n = 8
    x = np.ascontiguousarray(inputs["x"], dtype=np.float32)
    nc = build_program(["setup", "s5", "ffn0", "attn", "ffn1"])
    consts = host_consts()
    in_maps = []
    for c in range(n):
        xs = x[c * NSEQ:(c + 1) * NSEQ].reshape(NTOK, DT, 128)
        m = {k: np.ascontiguousarray(inputs[k], dtype=np.float32) for k in WEIGHT_NAMES}
        m.update(consts)
        m["xT"] = np.ascontiguousarray(xs.transpose(1, 2, 0))
        in_maps.append(m)
    res = run_bass_kernel_spmd(nc, in_maps, core_ids=list(range(n)))
    outs = []
    for c in range(n):
        o = res.results[c]["outT"]
        outs.append(o.transpose(2, 0, 1).reshape(NSEQ, SEQ, D))
    return np.ascontiguousarray(np.concatenate(outs, axis=0), dtype=np.float32)


DILS = (1, 4, 16)
REL_BUCKETS = 32
REL_MAX_DIST = 2048


def _t5_bucket(dist):
    exact = REL_BUCKETS // 2
    d = np.maximum(dist, 1).astype(np.float32)
    large = exact + (np.log(d / exact) / math.log(REL_MAX_DIST / exact) * (REL_BUCKETS - exact)).astype(np.int64)
    large = np.minimum(large, REL_BUCKETS - 1)
    return np.where(dist < exact, dist, large).astype(np.int32)


def host_onehot():
    oh = np.zeros((33, 3, 384), np.float32)
    for g, dil in enumerate(DILS):
        for dp in range(384):
            delta = dp - 127
            if 0 <= delta <= 128:
                b = int(_t5_bucket(np.array([delta * dil]))[0])
                oh[b, g, dp] = 1.0
            else:
                oh[32, g, dp] = 1.0
    return oh


def phase_attn(C, sb):
    P = C.P
    h1T, hcT = C.d["h1T"], C.d["hcT"]
    wq_d = C.d["attn_w_q"][0].rearrange("(dt p) c -> p dt c", p=128)
    wkv_d = C.d["attn_w_kv"].rearrange("(dt p) c -> p dt c", p=128)
    wo_d = C.d["attn_w_out"][0].rearrange("(hp p) c -> p hp c", p=128)
    ebf_t = C.d_handles["ebf"]
    ebf = C.d["ebf"]
    hT = sb("hT", [128, DT, SEQ], BF16)
    OT = sb("OT", [128, 8, SEQ], BF16)
    NUM1 = sb("NUM", [128, SEQ], F32)
    DEN1 = sb("DEN", [128, SEQ], F32)
    NUM = [NUM1, NUM1]
    DEN = [DEN1, DEN1]
    QTz = [[sb("QTz%d_%d" % (i, h), [128, SEQ], BF16) for h in range(2)] for i in range(2)]
    KT = [sb("KT%d" % i, [128, SEQ], BF16) for i in range(2)]
    Vz = [[sb("Vz%d_%d" % (i, h), [128, 16, 128], BF16) for h in range(2)] for i in range(2)]
    wq = [sb("wq%d" % i, [128, DT, 128], BF16) for i in range(2)]
    wk = [sb("wk%d" % i, [128, DT, 128], BF16) for i in range(2)]
    wv = [sb("wv%d" % i, [128, DT, 128], BF16) for i in range(2)]
    wo = sb("wo", [128, 8, D], BF16)
    ebt = [sb("ebt%d" % i, [128, 4, 128], F32) for i in range(2)]
    ebu = [sb("ebu%d" % i, [128, 4, 128], F32) for i in range(2)]
    Eb = [sb("Eb%d" % i, [128, 512], BF16) for i in range(2)]
    Pm = [sb("Pm%d" % i, [128, 512], BF16) for i in range(2)]
    onesz = [sb("onesz%d" % i, [128, 128], BF16) for i in range(2)]
    antiI = sb("antiI_sb", [128, 128], F32)
    rb33 = sb("rb33", [33, 48], F32)
    oh = sb("oh", [33, 3, 384], F32)
    fe = sb("fe", [16, 384], F32)
    hres = sb("hres", [128, DT, NB], F32)
    hout = sb("hout", [128, DT, NB], F32)
    lt = ln_tiles(sb)
    dma(P, antiI[:], C.d["antiI"], reads=[], writes=["antiI"])
    for i in range(2):
        for h in range(2):
            memset(P, "pool", Vz[i][h][:], 0.0, writes=[("Vz", i, h)])
            memset(P, "pool", QTz[i][h][:], 0.0, writes=[("QT", i)])
        memset(P, "pool", onesz[i][:], 0.0, writes=[("onesz", i)])
        memset(P, "pool", onesz[i][:, i * 64:(i + 1) * 64], 1.0, writes=[("onesz", i)])
    for hp in range(8):
        dma(P, wo[:, hp, :], wo_d[:, hp, :], reads=[], writes=[("wo", hp)], eng="pool")
    memset(P, "dve", rb33[:], -30000.0, writes=["rb33"])
    dma(P, rb33[0:32, :], C.d["rel_bias"], reads=[], writes=["rb33"])
    dma(P, oh[:], C.d["onehot"], reads=[], writes=["oh"])
    for g in range(3):
        mm(P, C.ps[0][0:16, 0:384], rb33[:, g * 16:(g + 1) * 16], oh[:, g, :], True, True, reads=["rb33", "oh"], writes=["ps0"])
        act(P, fe[:], C.ps[0][0:16, 0:384], AF.Exp, reads=["ps0"], writes=["fe"])
        dma(P, ebf[g * 16:(g + 1) * 16, :], fe[:], reads=["fe"], writes=["ebf"])

    iters = [(s, hp, g) for s in range(NSEQ) for hp in range(8) for g in range(3)][:DBG.get("niter", 48)]

    def proj_units(i):
        s, hp, g = iters[i]
        dil = DILS[g]
        nbc = 16 // dil
        pi = i % 2
        col = g * 1024 + hp * 128
        units = []

        def u0():
            if hp == 0 and g == 0:
                t0 = s * SEQ
                for dt in range(DT):
                    dma(P, hT[:, dt, :], h1T[dt, :, t0:t0 + SEQ], reads=[("h1T", b) for b in range(s * BPS, (s + 1) * BPS)],
                        writes=[("hT", dt)], eng="pool")
            dma(P, wq[pi][:], wq_d[:, :, col:col + 128], reads=[], writes=[("wq", pi)], eng="pool")
            dma(P, wk[pi][:], wkv_d[:, :, col:col + 128], reads=[], writes=[("wk", pi)], eng="pool")
            dma(P, wv[pi][:], wkv_d[:, :, 3072 + col:3072 + col + 128], reads=[], writes=[("wv", pi)], eng="pool")
            for h2 in range(2):
                head = g * 16 + hp * 2 + h2
                for pc in range(2):
                    off = head * 384 + (128 if pc == 0 else 0)
                    src = bass.AP(tensor=ebf_t, offset=off, ap=[[1, 128], [1, 128]])
                    dma(P, ebu[pi][:, h2 * 2 + pc, :], src, reads=["ebf"], writes=[("ebu", pi)])
            mm(P, C.ps[5][:, 0:512], antiI[:], ebu[pi][:].rearrange("p a b -> p (a b)"), True, True,
               reads=[("ebu", pi), "antiI"], writes=["ps5"])
            cp(P, "dve", ebt[pi][:].rearrange("p a b -> p (a b)"), C.ps[5][:, 0:512], reads=["ps5"], writes=[("ebt", pi)])
        units.append(u0)
        mpc = 512 // dil
        for which in range(2):
            for tb in range(4):
                def uqk(which=which, tb=tb):
                    w = (wq if which == 0 else wk)[pi]
                    wkey = ("wq" if which == 0 else "wk", pi)
                    key = ("QT" if which == 0 else "KT", pi)
                    pb = C.ps[tb % 2]
                    pk = "ps%d" % (tb % 2)
                    for dt in range(DT):
                        mm(P, pb[:, 0:512], w[:, dt, :], hT[:, dt, tb * 512:(tb + 1) * 512], dt == 0, dt == DT - 1,
                           reads=[wkey, ("hT", dt)], writes=[pk])
                    if which == 0:
                        for h2 in range(2):
                            src = pb[h2 * 64:(h2 + 1) * 64, 0:512].rearrange("p (m r) -> p r m", r=dil)
                            dst = QTz[pi][h2][h2 * 64:(h2 + 1) * 64, :].rearrange("p (r m) -> p r m", r=dil)[:, :, tb * mpc:(tb + 1) * mpc]
                            act(P, dst, src, AF.Copy, reads=[pk], writes=[key], scale=0.125)
                    else:
                        src = pb[:, 0:512].rearrange("p (m r) -> p r m", r=dil)
                        dst = KT[pi][:].rearrange("p (r m) -> p r m", r=dil)[:, :, tb * mpc:(tb + 1) * mpc]
                        cp(P, "dve", dst, src, reads=[pk], writes=[key])
                units.append(uqk)
        for kb in range(16):
            def uv(kb=kb):
                r, n = kb // nbc, kb % nbc
                start = n * 128 * dil + r
                pb = C.ps[2]
                sub = kb % 4
                for dt in range(DT):
                    if dil == 1:
                        lhs = hT[:, dt, start:start + 128]
                    else:
                        lhs = hT[:, dt, start - r:start - r + 128 * dil].rearrange("p (m r) -> p r m", r=dil)[:, r, :]
                    mm(P, pb[:, sub * 128:(sub + 1) * 128], lhs, wv[pi][:, dt, :], dt == 0, dt == DT - 1,
                       reads=[("wv", pi), ("hT", dt)], writes=["ps2"])
                cp(P, DBG.get("veng", "act"), Vz[pi][0][:, kb, 0:64], pb[:, sub * 128:sub * 128 + 64], reads=["ps2"], writes=[("Vz", pi, 0)])
                cp(P, DBG.get("veng", "act"), Vz[pi][1][:, kb, 64:128], pb[:, sub * 128 + 64:(sub + 1) * 128], reads=["ps2"], writes=[("Vz", pi, 1)])
            units.append(uv)
        return units

    def scores(i, qb):
        s, hp, g = iters[i]
        dil = DILS[g]
        nbc = 16 // dil
        pi = i % 2
        n = qb % nbc
        pcs = [1] if n == 0 else [0, 1]
        e, ek = Eb[qb % 2], "Eb%d" % (qb % 2)
        pm, pmk = Pm[qb % 2], "Pm%d" % (qb % 2)
        eb, ebk = ebt[pi], ("ebt", pi)
        sps = C.ps[3 + qb % 2]
        spk = "ps%d" % (3 + qb % 2)
        last = (1, 1)
        for h2 in range(2):
            for pc in pcs:
                kb = qb if pc == 1 else qb - 1
                slot = h2 * 2 + pc
                mm(P, sps[:, slot * 128:(slot + 1) * 128], KT[pi][:, kb * 128:(kb + 1) * 128],
                   QTz[pi][h2][:, qb * 128:(qb + 1) * 128], True, True,
                   reads=[("QT", pi), ("KT", pi)], writes=[spk], inc=((h2, pc) == last))
        if n == 0:
            for h2 in range(2):
                c0 = h2 * 256 + 128
                act(P, e[:, c0:c0 + 128], sps[:, c0:c0 + 128], AF.Exp, reads=[spk], writes=[ek])
        else:
            act(P, e[:], sps[:, 0:512], AF.Exp, reads=[spk], writes=[ek])
        if n == 0:
            for h2 in range(2):
                tt(P, "dve", pm[:, h2 * 256 + 128:(h2 + 1) * 256], e[:, h2 * 256 + 128:(h2 + 1) * 256], eb[:, h2 * 2 + 1, :],
                   ALU.mult, reads=[ek, ebk], writes=[pmk])
        else:
            tt(P, "dve", pm[:], e[:], eb[:].rearrange("p a b -> p (a b)"), ALU.mult, reads=[ek, ebk], writes=[pmk])

    def pv(i, qb):
        s, hp, g = iters[i]
        dil = DILS[g]
        nbc = 16 // dil
        pi = i % 2
        r, n = qb // nbc, qb % nbc
        start = n * 128 * dil + r
        pcs = [1] if n == 0 else [0, 1]
        pm, pmk = Pm[qb % 2], "Pm%d" % (qb % 2)
        nd = C.ps[7 - qb % 2][:, 0:256]
        ndk = "ps%d" % (7 - qb % 2)
        combos = [(h2, pc) for h2 in range(2) for pc in pcs]
        for part in range(2):
            for ci, (h2, pc) in enumerate(combos):
                kb = qb if pc == 1 else qb - 1
                slot = h2 * 2 + pc
                lhs = Vz[pi][h2][:, kb, :] if part == 0 else onesz[h2][:]
                mm(P, nd[:, part * 128:(part + 1) * 128], lhs, pm[:, slot * 128:(slot + 1) * 128],
                   ci == 0, ci == len(combos) - 1,
                   reads=[pmk, ("Vz", pi, h2), ("onesz", h2)], writes=[ndk], inc=(part == 1 and ci == len(combos) - 1))
        nb = (s * 8 + hp) % 2
        nk, dk = ("NUM", nb), ("DEN", nb)
        if dil > 1:
            numv = NUM[nb][:, start - r:start - r + 128 * dil].rearrange("p (m r) -> p r m", r=dil)[:, r, :]
            denv = DEN[nb][:, start - r:start - r + 128 * dil].rearrange("p (m r) -> p r m", r=dil)[:, r, :]
        else:
            numv = NUM[nb][:, start:start + 128]
            denv = DEN[nb][:, start:start + 128]
        if g == 0:
            cp(P, "dve", numv, nd[:, 0:128], reads=[ndk], writes=[nk])
            cp(P, "dve", denv, nd[:, 128:256], reads=[ndk], writes=[dk])
        else:
            tt(P, "dve", numv, nd[:, 0:128], numv, ALU.add, reads=[ndk, nk], writes=[nk])
            tt(P, "dve", denv, nd[:, 128:256], denv, ALU.add, reads=[ndk, dk], writes=[dk])

    def out_proj(s):
        for bl in range(BPS):
            b = s * BPS + bl
            hk = [("hres", dt) for dt in range(DT)]
            dma(P, hres[:], h1T[:, :, b * NB:(b + 1) * NB].rearrange("d p t -> p d t"), reads=[("h1T", b)], writes=hk)
            for dt in range(DT):
                pb = C.ps[dt % 2]
                pk = "ps%d" % (dt % 2)
                for hp in range(8):
                    mm(P, pb[:, 0:NB], wo[:, hp, dt * 128:(dt + 1) * 128], OT[:, hp, bl * NB:(bl + 1) * NB], hp == 0, hp == 7,
                       reads=[("wo", hp), ("OT", hp)], writes=[pk])
                stt(P, hres[:, dt, :], hres[:, dt, :], ALPHA, pb[:, 0:NB], ALU.mult, ALU.add, reads=[hk[dt], pk], writes=[hk[dt]])
            houtk = [("hout", dt) for dt in range(DT)]
            layer_norm_block(C, lt, hres, hk, hout, houtk, 2)
            dma(P, hcT[:, :, b * NB:(b + 1) * NB].rearrange("d p t -> p d t"), hout[:], reads=houtk, writes=[("hcT", b)])

    for u in proj_units(0):
        u()
    for i in range(len(iters)):
        s, hp, g = iters[i]
        nxt = proj_units(i + 1) if i + 1 < len(iters) else []
        ui = 0
        for qb in range(16):
            scores(i, qb)
            if DBG.get("nopipe"):
                pv(i, qb)
            elif qb >= 1:
                pv(i, qb - 1)
            for _ in range(2):
                if ui < len(nxt):
                    nxt[ui]()
                    ui += 1
        if not DBG.get("nopipe"):
            pv(i, 15)
        while ui < len(nxt):
            nxt[ui]()
            ui += 1
        if g == 2:
            nb = (s * 8 + hp) % 2
            for q4 in range(4):
                sl = slice(q4 * 512, (q4 + 1) * 512)
                P.op("dve", lambda e, o=DEN[nb][:, sl]: e.reciprocal(out=o, in_=o), reads=[("DEN", nb)], writes=[("DEN", nb)])
                tt(P, "pool", OT[:, hp, sl], NUM[nb][:, sl], DEN[nb][:, sl], ALU.mult, reads=[("NUM", nb), ("DEN", nb)], writes=[("OT", hp)])
            if hp == 7:
                out_proj(s)


TC = 64
NCH = SEQ // TC


def phase_s5(C, sb):
    P = C.P
    W = 2 * TC
    xT, haT = C.d["xT"], C.d["haT"]
    tmp = sb("s5tmp", [128, 128], F32)
    sm = {n: sb("s5_" + n, [128, 32], F32) for n in
          ("lr", "li", "ldt", "dt", "wr", "wi", "ar", "ai", "t1", "t2", "t3", "t4", "er", "ei", "den", "cor", "coi", "nr")}
    scr = sb("s5scr", [128, 6144], F32)

    def v16(j):
        return scr[:, 2048 + j * 512:2048 + (j + 1) * 512].rearrange("p (g h) -> p g h", h=16)
    Bn = [v16(0), v16(1)]
    Bb = [v16(2), v16(3)]
    bt = [v16(4), v16(5)]
    Z = sb("Zs5", [128, 128], F32)
    Bblk = [sb("Bblk%d" % i, [128, 32, 128], BF16) for i in range(2)]
    Cblk = [sb("Cblk%d" % i, [128, 32, 128], BF16) for i in range(2)]
    xcb = [sb("xcb%d" % i, [128, DT, W], BF16) for i in range(2)]
    hb16 = [sb("hb16_%d" % i, [128, 32, W], BF16) for i in range(2)]
    Cnat1 = scr[0:16, 0:2048].rearrange("p (a b) -> p a b", b=64)
    Cnat = [Cnat1, Cnat1]
    dcol = sb("dcol", [128, 8], F32)
    wglu = sb("wglu", [128, DT, D], BF16)
    wout = sb("wout", [128, DT, D], BF16)
    xc = [sb("xc%d" % i, [128, DT, W], F32) for i in range(2)]
    BU1 = [sb("BU_%d" % j, [128, 32, W], F32) for j in range(2)]
    cosT = sb("s5cos", [128, 32, TC], F32)
    sinT = sb("s5sin", [128, 32, TC], F32)
    rT = sb("s5rT", [128, 32, TC], F32)
    T1 = scr[:, 0:2048].rearrange("p (g t) -> p g t", t=TC)
    T2 = scr[:, 2048:4096].rearrange("p (g t) -> p g t", t=TC)
    T3 = scr[:, 4096:6144].rearrange("p (g t) -> p g t", t=TC)
    hprev = [[sb("s5hp%d_%d" % (s_, j), [128, 32], F32) for j in range(2)] for s_ in range(2)]
    BU = [BU1, BU1]
    ys = sb("ys", [128, DT, W], F32)
    ygf = sb("ygf", [128, DT, W], F32)
    ygb = sb("ygb", [128, DT, W], BF16)
    sig = [sb("sig%d" % i, [128, W], F32) for i in range(2)]
    gb = sb("gb", [128, DT, W], BF16)
    hout = sb("s5hout", [128, DT, W], F32)
    lt = ln_tiles(lambda n, sh, d: sb("s5" + n, sh, d), NB=W)
    k = lambda n: "s5_" + n

    def v(op, out, a, b, eng="dve"):
        tt(P, eng, sm[out][:], sm[a][:], sm[b][:], op, reads=[k(a), k(b)], writes=[k(out)])

    def vs(out, a, s1, s2=None, op0=ALU.mult, op1=ALU.add):
        ts(P, "dve", sm[out][:], sm[a][:], s1, s2, op0, op1 if s2 is not None else ALU.bypass, reads=[k(a)], writes=[k(out)])

    load_cols(C, tmp, sm["lr"][:], C.d["s5_lam_re"][0].rearrange("(gp g2) p -> gp (g2 p)", g2=2), 32, k("lr"))
    load_cols(C, tmp, sm["li"][:], C.d["s5_lam_im"][0].rearrange("(gp g2) p -> gp (g2 p)", g2=2), 32, k("li"))
    ld2 = sb("ld2", [32, 2], F32)
    ldn = sb("ldn", [32, 128], F32)
    dma(P, ld2[:], C.d["s5_log_dt"][0].rearrange("(gp g2) -> gp g2", g2=2), reads=[], writes=["ld2"])
    for g2 in range(2):
        ts(P, "dve", ldn[:, g2 * 64:(g2 + 1) * 64], C.ident[0:32, 0:64], 0.0, ld2[:, g2:g2 + 1], ALU.mult, ALU.add,
           reads=["ident", "ld2"], writes=["ldn"])
    tp(P, C.ps[7][:, 0:32], ldn[:], C.ident[0:32, 0:32], reads=["ldn", "ident"], writes=["ps7"])
    cp(P, "dve", sm["ldt"][:], C.ps[7][:, 0:32], reads=["ps7"], writes=[k("ldt")])
    act(P, sm["dt"][:], sm["ldt"][:], AF.Exp, reads=[k("ldt")], writes=[k("dt")])
    v(ALU.mult, "wr", "lr", "dt"); vs("wr", "wr", 1.0 / 64)
    v(ALU.mult, "wi", "li", "dt"); vs("wi", "wi", 1.0 / 64)
    vs("ar", "wr", 1.0 / 8, 1.0)
    vs("ai", "wi", 1.0 / 8)
    for kk in range(7, 0, -1):
        v(ALU.mult, "t1", "ar", "wr"); v(ALU.mult, "t2", "ai", "wi"); v(ALU.subtract, "t1", "t1", "t2")
        v(ALU.mult, "t3", "ar", "wi"); v(ALU.mult, "t4", "ai", "wr"); v(ALU.add, "t3", "t3", "t4")
        vs("ar", "t1", 1.0 / kk, 1.0)
        vs("ai", "t3", 1.0 / kk)
    for _ in range(6):
        v(ALU.mult, "t1", "ar", "ar"); v(ALU.mult, "t2", "ai", "ai"); v(ALU.mult, "t3", "ar", "ai")
        v(ALU.subtract, "ar", "t1", "t2"); vs("ai", "t3", 2.0)
    v(ALU.mult, "t1", "lr", "dt")
    act(P, sm["er"][:], sm["t1"][:], AF.Exp, reads=[k("t1")], writes=[k("er")])
    P.op("dve", lambda e, o=sm["ei"][:], i_=sm["er"][:]: e.reciprocal(out=o, in_=i_), reads=[k("er")], writes=[k("ei")])
    v(ALU.mult, "t3", "ar", "ei")
    v(ALU.mult, "t4", "ai", "ei")
    cp(P, "dve", cosT[:, :, 0], sm["t3"][:], reads=[k("t3")], writes=["s5cos"])
    cp(P, "dve", sinT[:, :, 0], sm["t4"][:], reads=[k("t4")], writes=["s5sin"])
    memset(P, "dve", rT[:, :, 0], 0.0, writes=["s5rT"])
    cp(P, "dve", rT[:, :, 1:TC], sm["er"][:].unsqueeze(2).to_broadcast([128, 32, TC - 1]), reads=[k("er")], writes=["s5rT"])
    ysk = [("ys", ct) for ct in range(DT)]
    ygk = [("ygf", ct) for ct in range(DT)]
    n_ = 1
    while n_ < TC:
        ta = ys[:].rearrange("p a b -> p (a b)")[:, 0:32 * n_].rearrange("p (g t) -> p g t", t=n_)
        tb_ = ygf[:].rearrange("p a b -> p (a b)")[:, 0:32 * n_].rearrange("p (g t) -> p g t", t=n_)
        pr = cosT[:, :, n_ - 1:n_].to_broadcast([128, 32, n_])
        pi = sinT[:, :, n_ - 1:n_].to_broadcast([128, 32, n_])
        c_old, s_old = cosT[:, :, 0:n_], sinT[:, :, 0:n_]
        tt(P, "dve", ta, c_old, pr, ALU.mult, reads=["s5cos"], writes=ysk)
        tt(P, "dve", tb_, s_old, pi, ALU.mult, reads=["s5sin"], writes=ygk)
        tt(P, "dve", cosT[:, :, n_:2 * n_], ta, tb_, ALU.subtract, reads=ysk + ygk, writes=["s5cos"])
        tt(P, "dve", ta, c_old, pi, ALU.mult, reads=["s5cos", "s5sin"], writes=ysk)
        tt(P, "dve", tb_, s_old, pr, ALU.mult, reads=["s5sin", "s5cos"], writes=ygk)
        tt(P, "dve", sinT[:, :, n_:2 * n_], ta, tb_, ALU.add, reads=ysk + ygk, writes=["s5sin"])
        n_ *= 2
    for s_ in range(2):
        for j in range(2):
            memset(P, "dve", hprev[s_][j][:], 0.0, writes=[("hp", s_, j)])
    vs("nr", "ar", 1.0, -1.0)
    v(ALU.mult, "t1", "lr", "lr"); v(ALU.mult, "t2", "li", "li"); v(ALU.add, "den", "t1", "t2")
    P.op("dve", lambda e, o=sm["den"][:]: e.reciprocal(out=o, in_=o), reads=[k("den")], writes=[k("den")])
    v(ALU.mult, "t1", "nr", "lr"); v(ALU.mult, "t2", "ai", "li"); v(ALU.add, "t1", "t1", "t2"); v(ALU.mult, "cor", "t1", "den")
    v(ALU.mult, "t3", "ai", "lr"); v(ALU.mult, "t4", "nr", "li"); v(ALU.subtract, "t3", "t3", "t4"); v(ALU.mult, "coi", "t3", "den")
    bsrc = [C.d["s5_b_re"][0], C.d["s5_b_im"][0]]
    for i in range(2):
        for half in range(2):
            dma(P, Bn[i][:, half * 16:(half + 1) * 16, :],
                bsrc[i].rearrange("(gp g2) p h -> (g2 p) gp h", g2=2)[:, half * 16:(half + 1) * 16, :], reads=[], writes=[("Bn", i)])
    cor16 = v16(6)
    coi16 = v16(7)
    for h in range(16):
        cp(P, "dve", cor16[:, :, h], sm["cor"][:], reads=[k("cor")], writes=["cor16"])
        cp(P, "pool", coi16[:, :, h], sm["coi"][:], reads=[k("coi")], writes=["coi16"])
    tt(P, "dve", bt[0][:], cor16[:], Bn[0][:], ALU.mult, reads=["cor16", ("Bn", 0)], writes=["bt0"])
    tt(P, "dve", bt[1][:], coi16[:], Bn[1][:], ALU.mult, reads=["coi16", ("Bn", 1)], writes=["bt1"])
    tt(P, "dve", Bb[0][:], bt[0][:], bt[1][:], ALU.subtract, reads=["bt0", "bt1"], writes=[("Bb", 0)])
    tt(P, "dve", bt[0][:], cor16[:], Bn[1][:], ALU.mult, reads=["cor16", ("Bn", 1), ("Bb", 0)], writes=["bt0"])
    tt(P, "dve", bt[1][:], coi16[:], Bn[0][:], ALU.mult, reads=["coi16", ("Bn", 0), ("Bb", 0)], writes=["bt1"])
    tt(P, "dve", Bb[1][:], bt[0][:], bt[1][:], ALU.add, reads=["bt0", "bt1"], writes=[("Bb", 1)])
    for i in range(2):
        for gp in range(32):
            c0 = ((2 * gp) % 8) * 16
            memset(P, "pool", Z[:], 0.0, writes=["Zs5"])
            cp(P, "pool", Z[0:64, c0:c0 + 16], Bb[i][0:64, gp, :], reads=[("Bb", i)], writes=["Zs5"])
            cp(P, "pool", Z[64:128, c0 + 16:c0 + 32], Bb[i][64:128, gp, :], reads=[("Bb", i)], writes=["Zs5"])
            pb = C.ps[gp % 2]
            tp(P, pb[:, 0:128], Z[:], C.ident[:], reads=["Zs5", "ident"], writes=["ps%d" % (gp % 2)])
            cp(P, "act", Bblk[i][:, gp, :], pb[:, 0:128], reads=["ps%d" % (gp % 2)], writes=[("Bblk", i)])
    csrc = [C.d["s5_c_re"][0], C.d["s5_c_im"][0]]
    for i in range(2):
        memset(P, "pool", Cblk[i][:], 0.0, writes=[("Cblk", i)])
        for half in range(2):
            dma(P, Cnat[i][:], csrc[i].rearrange("g h p -> h g p")[:, half * 32:(half + 1) * 32, :], reads=[], writes=["Cnat"])
            for gp in range(half * 16, (half + 1) * 16):
                c0 = ((2 * gp) % 8) * 16
                gl = 2 * gp - half * 32
                pb = C.ps[2 + gp % 2]
                pk = "ps%d" % (2 + gp % 2)
                tp(P, pb[:, 0:16], Cnat[i][:, gl:gl + 2, :].rearrange("h a p -> h (a p)"), C.ident[0:16, 0:16],
                   reads=["Cnat", "ident"], writes=[pk])
                sc = 1.0 if i == 0 else -1.0
                act(P, Cblk[i][0:64, gp, c0:c0 + 16], pb[0:64, 0:16], AF.Copy, reads=[pk], writes=[("Cblk", i)], scale=sc)
                act(P, Cblk[i][64:128, gp, c0 + 16:c0 + 32], pb[64:128, 0:16], AF.Copy, reads=[pk], writes=[("Cblk", i)], scale=sc)
    fence = sb("s5fence", [128, 1], F32)
    memset(P, "dve", fence[:], 0.0,
           writes=["s5fence", "s5T1", "s5T2", "s5T3", "Cnat", ("Bn", 0), ("Bn", 1), ("Bb", 0), ("Bb", 1), "bt0", "bt1", "cor16", "coi16"])
    load_cols(C, tmp, dcol[:], C.d["s5_d"][0].rearrange("(a b) h -> a (b h)", b=8), 8, "dcol")
    for dt in range(DT):
        dma(P, wglu[:, dt, :], C.d["s5_w_glu"][0][dt * 128:(dt + 1) * 128, :], reads=[], writes=[("wglu", dt)], eng="pool")
        dma(P, wout[:, dt, :], C.d["s5_w_out"][0][dt * 128:(dt + 1) * 128, :], reads=[], writes=[("wout", dt)], eng="pool")

    dump = DBG.get("s5dump")
    if dump:
        dbg = C.d["dbg"]
        dma(P, dbg[0, :, 0:32], sm["ar"][:], reads=[k("ar")], writes=["dbg0"])
        dma(P, dbg[0, :, 32:64], sm["ai"][:], reads=[k("ai")], writes=["dbg0"])
        dma(P, dbg[0, :, 64:96], sm["cor"][:], reads=[k("cor")], writes=["dbg0"])
        dma(P, dbg[0, :, 96:128], sm["coi"][:], reads=[k("coi")], writes=["dbg0"])
        dma(P, dbg[0, :, 128:160], sm["dt"][:], reads=[k("dt")], writes=["dbg0"])
        dma(P, dbg[1, :, 0:512], Bb[0][:].rearrange("p a b -> p (a b)"), reads=[("Bb", 0)], writes=["dbg1"])
        dma(P, dbg[1, :, 512:1024], Bb[1][:].rearrange("p a b -> p (a b)"), reads=[("Bb", 1)], writes=["dbg1"])
        dma(P, dbg[5, :, 0:4096], Bblk[0][:].rearrange("p a b -> p (a b)"), reads=[("Bblk", 0)], writes=["dbg5"])
        dma(P, dbg[6, :, 0:4096], Cblk[0][:].rearrange("p a b -> p (a b)"), reads=[("Cblk", 0)], writes=["dbg6"])
    nch = DBG.get("nch", NCH)
    buk = [("BU", 0), ("BU", 1)]
    bu = BU[0]
    fl = lambda t_: t_[:].rearrange("p g t -> p (g t)")

    def load_x(c):
        for s_ in range(2):
            tok = s_ * SEQ + c * TC
            dma(P, xc[c % 2][:, :, s_ * TC:(s_ + 1) * TC], xT[:, :, tok:tok + TC].rearrange("d p t -> p d t"), reads=[],
                writes=["xc%d" % (c % 2)])
            dma(P, xcb[c % 2][:, :, s_ * TC:(s_ + 1) * TC], xT[:, :, tok:tok + TC].rearrange("d p t -> p d t"), reads=[],
                writes=["xcb%d" % (c % 2)], eng="pool")

    def bu_stage(c):
        xb, xbk = xcb[c % 2], "xcb%d" % (c % 2)
        for g4 in range(8):
            for i in range(2):
                pb = C.ps[i]
                for j in range(4):
                    gp = g4 * 4 + j
                    mm(P, pb[:, j * W:(j + 1) * W], Bblk[i][:, gp, :], xb[:, gp // 4, :], True, True,
                       reads=[("Bblk", i), xbk], writes=["ps%d" % i], inc=(j == 3))
                cp(P, "act", bu[i][:, g4 * 4:(g4 + 1) * 4, :].rearrange("p a b -> p (a b)"), pb[:, 0:4 * W], reads=["ps%d" % i], writes=[buk[i]])

    def scan_gen(c):
        for s_ in range(2):
            X = bu[0][:, :, s_ * TC:(s_ + 1) * TC]
            Y = bu[1][:, :, s_ * TC:(s_ + 1) * TC]
            hp_r, hp_i = hprev[s_]
            kr, ki = ("hp", s_, 0), ("hp", s_, 1)
            tt(P, "dve", T1[:], cosT[:], X, ALU.mult, reads=["s5cos", buk[0]], writes=["s5T1"])
            tt(P, "dve", T2[:], sinT[:], Y, ALU.mult, reads=["s5sin", buk[1]], writes=["s5T2"])
            tt(P, "dve", T1[:], T1[:], T2[:], ALU.add, reads=["s5T1", "s5T2"], writes=["s5T1"])
            yield
            tt(P, "dve", T2[:], cosT[:], Y, ALU.mult, reads=["s5cos", buk[1]], writes=["s5T2"])
            tt(P, "dve", T3[:], sinT[:], X, ALU.mult, reads=["s5sin", buk[0]], writes=["s5T3"])
            tt(P, "dve", T2[:], T2[:], T3[:], ALU.subtract, reads=["s5T2", "s5T3"], writes=["s5T2"])
            yield
            tt(P, "dve", sm["t1"][:], sm["er"][:], hp_r[:], ALU.mult, reads=[k("er"), kr], writes=[k("t1")])
            tt(P, "dve", T1[:, :, 0], T1[:, :, 0], sm["t1"][:], ALU.add, reads=["s5T1", k("t1")], writes=["s5T1"])
            tt(P, "dve", sm["t2"][:], sm["er"][:], hp_i[:], ALU.mult, reads=[k("er"), ki], writes=[k("t2")])
            tt(P, "dve", T2[:, :, 0], T2[:, :, 0], sm["t2"][:], ALU.add, reads=["s5T2", k("t2")], writes=["s5T2"])
            P.op("dve", lambda e, o=fl(T3), d0=fl(rT), d1=fl(T1): e.tensor_tensor_scan(out=o, data0=d0, data1=d1, initial=0.0,
                                                                                      op0=ALU.mult, op1=ALU.add),
                 reads=["s5rT", "s5T1"], writes=["s5T3"])
            yield
            P.op("dve", lambda e, o=fl(T1), d0=fl(rT), d1=fl(T2): e.tensor_tensor_scan(out=o, data0=d0, data1=d1, initial=0.0,
                                                                                      op0=ALU.mult, op1=ALU.add),
                 reads=["s5rT", "s5T2"], writes=["s5T1"])
            yield
            tt(P, "dve", X, cosT[:], T3[:], ALU.mult, reads=["s5cos", "s5T3"], writes=[buk[0]])
            tt(P, "dve", T2[:], sinT[:], T1[:], ALU.mult, reads=["s5sin", "s5T1"], writes=["s5T2"])
            tt(P, "dve", X, X, T2[:], ALU.subtract, reads=[buk[0], "s5T2"], writes=[buk[0]])
            yield
            tt(P, "dve", Y, sinT[:], T3[:], ALU.mult, reads=["s5sin", "s5T3"], writes=[buk[1]])
            tt(P, "dve", T2[:], cosT[:], T1[:], ALU.mult, reads=["s5cos", "s5T1"], writes=["s5T2"])
            tt(P, "dve", Y, Y, T2[:], ALU.add, reads=[buk[1], "s5T2"], writes=[buk[1]])
            cp(P, "dve", hp_r[:], X[:, :, TC - 1], reads=[buk[0]], writes=[kr])
            cp(P, "dve", hp_i[:], Y[:, :, TC - 1], reads=[buk[1]], writes=[ki])
            yield

    def cast_stage(c):
        for i in range(2):
            cp(P, "act", hb16[i][:], bu[i][:], reads=[buk[i]], writes=[("hb16", i)])

    def tail_gen(c):
        x, xk = xc[c % 2], "xc%d" % (c % 2)
        for ct in range(DT):
            pb = C.ps[2 + ct % 2]
            pk = "ps%d" % (2 + ct % 2)
            n = 0
            for gp in range(4 * ct, 4 * ct + 4):
                for i in range(2):
                    mm(P, pb[:, 0:W], Cblk[i][:, gp, :], hb16[i][:, gp, :], n == 0, n == 7, reads=[("Cblk", i), ("hb16", i)], writes=[pk])
                    n += 1
            stt(P, ys[:, ct, :], x[:, ct, :], dcol[:, ct:ct + 1], pb[:, 0:W], ALU.mult, ALU.add, reads=[xk, "dcol", pk], writes=[("ys", ct)])
            act(P, ygf[:, ct, :], ys[:, ct, :], AF.Gelu_apprx_tanh, reads=[("ys", ct)], writes=[("ygf", ct)])
            cp(P, "pool", ygb[:, ct, :], ygf[:, ct, :], reads=[("ygf", ct)], writes=[("ygb", ct)])
            yield
        for ct in range(DT):
            bank = 4 if ct % 2 == 0 else 7
            pb = C.ps[bank]
            pk = "ps%d" % bank
            for kt in range(DT):
                mm(P, pb[:, 0:W], wglu[:, kt, ct * 128:(ct + 1) * 128], ygb[:, kt, :], kt == 0, kt == DT - 1,
                   reads=[("wglu", kt), ("ygb", kt)], writes=[pk])
            sg = sig[ct % 2]
            sgk = "sig%d" % (ct % 2)
            act(P, sg[:], pb[:, 0:W], AF.Sigmoid, reads=[pk, "bglu"], writes=[sgk], bias=C.bglu[:, ct:ct + 1])
            tt(P, "dve", gb[:, ct, :], ygf[:, ct, :], sg[:], ALU.mult, reads=[("ygf", ct), sgk], writes=[("gb", ct)])
            yield
        for dt in range(DT):
            pb = C.ps[2 + dt % 2]
            pk = "ps%d" % (2 + dt % 2)
            for kt in range(DT):
                mm(P, pb[:, 0:W], wout[:, kt, dt * 128:(dt + 1) * 128], gb[:, kt, :], kt == 0, kt == DT - 1,
                   reads=[("wout", kt), ("gb", kt)], writes=[pk])
            stt(P, ys[:, dt, :], x[:, dt, :], ALPHA, pb[:, 0:W], ALU.mult, ALU.add, reads=[xk, pk], writes=[("ys", dt)])
            yield
        houtk = [("s5hout", dt) for dt in range(DT)]
        for _ in ln_gen(C, lt, ys, [("ys", dt) for dt in range(DT)], hout, houtk, 0, NB=W):
            yield
        for s_ in range(2):
            tok = s_ * SEQ + c * TC
            b = tok // NB
            dma(P, haT[:, :, tok:tok + TC].rearrange("d p t -> p d t"), hout[:, :, s_ * TC:(s_ + 1) * TC], reads=houtk, writes=[("haT", b)])

    if nch > 0:
        load_x(0)
        bu_stage(0)
        for _ in scan_gen(0):
            pass
        cast_stage(0)
    for c in range(nch):
        if c + 1 < nch:
            load_x(c + 1)
            bu_stage(c + 1)
            sg, tg = scan_gen(c + 1), tail_gen(c)
            done_s = done_t = False
            while not (done_s and done_t):
                if not done_s:
                    done_s = next(sg, "end") == "end"
                for _ in range(4):
                    if not done_t:
                        done_t = next(tg, "end") == "end"
            cast_stage(c + 1)
        else:
            for _ in tail_gen(c):
                pass
```
